# Optimizing a Trainium2 kernel written in Bass

```python
import math
import jax
import jax.numpy as jnp
from jax import lax
import numpy as np

D_MODEL = 1024
BATCH = 8
SEQ = 4096
DEPTH = 4

CHUNK = 64
Q_BLOCK = 128
NORM_EPS = 1e-6
CONV_K = 4
D_FF = 2816
N_MEM = 256

SSD_HEADS = 8
SSD_HEAD_DIM = 64
SSD_WIDTH = SSD_HEADS * SSD_HEAD_DIM
SSD_GROUPS = 2
SSD_STATE = 64
SSD_CONV_DIM = SSD_WIDTH + 2 * SSD_GROUPS * SSD_STATE
SSD_IN = SSD_WIDTH + SSD_CONV_DIM + SSD_HEADS

MLA_HEADS = 4
MLA_Q_LORA = 256
MLA_KV_LORA = 128
MLA_NOPE = 64
MLA_ROPE = 32
MLA_V = 64
MLA_WIDTH = MLA_HEADS * MLA_V
MLA_IN = MLA_Q_LORA + MLA_KV_LORA + MLA_ROPE
ROPE_THETA = 10000.0

GDN_HEADS = 4
GDN_DK = 64
GDN_DV = 64
GDN_QK_WIDTH = GDN_HEADS * GDN_DK
GDN_WIDTH = GDN_HEADS * GDN_DV
GDN_CONV_DIM = 2 * GDN_QK_WIDTH + GDN_WIDTH
GDN_IN = GDN_CONV_DIM + GDN_WIDTH + 2 * GDN_HEADS

D_MIX = SSD_WIDTH + MLA_WIDTH + GDN_WIDTH
IN_COLS = SSD_IN + MLA_IN + GDN_IN

XA_HEADS = 4
XA_HEAD_DIM = D_MODEL // XA_HEADS

N_NORMS = 9
FFN1_PRE = 0
FFN1_POST = 1
MIX_PRE = 2
MIX_POST = 3
MEM_NORM = 4
XA_PRE = 5
XA_POST = 6
FFN2_PRE = 7
FFN2_POST = 8

kernel_name = 'hybrid_ssd_mla_gdn_macaron_trunk'


def rms_norm(x, g):
    xf = x.astype(jnp.float32)
    y = xf * lax.rsqrt(jnp.mean(xf * xf, axis=-1, keepdims=True) + NORM_EPS)
    return (y * g.astype(jnp.float32)).astype(x.dtype)


def l2_normalize(x):
    return x * lax.rsqrt(jnp.sum(x * x, axis=-1, keepdims=True) + NORM_EPS)


def swiglu_ffn(x, w_up, w_down):
    gate, up = jnp.split(x @ w_up, 2, axis=-1)
    return (jax.nn.silu(gate) * up) @ w_down


def causal_depthwise_conv(x, w):
    k = w.shape[0]
    xp = jnp.pad(x, ((0, 0), (k - 1, 0), (0, 0)))
    return lax.conv_general_dilated(xp, w[:, None, :].astype(x.dtype), (1,), 'VALID',
                                    dimension_numbers=('NWC', 'WIO', 'NWC'),
                                    feature_group_count=x.shape[-1])


def rope_tables(positions):
    inv = 1.0 / (ROPE_THETA ** (jnp.arange(0, MLA_ROPE, 2, dtype=jnp.float32) / MLA_ROPE))
    ang = positions.astype(jnp.float32)[..., None] * inv
    return jnp.cos(ang), jnp.sin(ang)


def apply_rope(x, cos, sin):
    xf = x.astype(jnp.float32)
    x1, x2 = jnp.split(xf, 2, axis=-1)
    return jnp.concatenate([x1 * cos - x2 * sin, x2 * cos + x1 * sin], axis=-1).astype(x.dtype)


def ssd_mixer(cols, conv_w, conv_b, dt_bias, a_log, d_skip, norm_g):
    f32 = jnp.float32
    bsz, s_len, _ = cols.shape
    nc = s_len // CHUNK
    hpg = SSD_HEADS // SSD_GROUPS
    z, xbc, dt = jnp.split(cols, [SSD_WIDTH, SSD_WIDTH + SSD_CONV_DIM], axis=-1)
    xbc = jax.nn.silu(causal_depthwise_conv(xbc, conv_w) + conv_b)
    xs, bm, cm = jnp.split(xbc, [SSD_WIDTH, SSD_WIDTH + SSD_GROUPS * SSD_STATE], axis=-1)
    xs = xs.reshape(bsz, nc, CHUNK, SSD_GROUPS, hpg, SSD_HEAD_DIM).astype(f32)
    bm = bm.reshape(bsz, nc, CHUNK, SSD_GROUPS, SSD_STATE).astype(f32)
    cm = cm.reshape(bsz, nc, CHUNK, SSD_GROUPS, SSD_STATE).astype(f32)
    dt = jax.nn.softplus(dt.astype(f32) + dt_bias.astype(f32))
    a = -jnp.exp(a_log.astype(f32))
    dt = dt.reshape(bsz, nc, CHUNK, SSD_GROUPS, hpg)
    xdt = xs * dt[..., None]
    a_cum = jnp.cumsum(dt * a.reshape(SSD_GROUPS, hpg), axis=2)
    causal = jnp.tril(jnp.ones((CHUNK, CHUNK), bool))[:, :, None, None]
    seg = a_cum[:, :, :, None] - a_cum[:, :, None, :]
    lmat = jnp.exp(jnp.where(causal, seg, -jnp.inf))
    cb = jnp.einsum('bclgn,bcsgn->bclsg', cm, bm)
    y_diag = jnp.einsum('bclsg,bclsgh,bcsghp->bclghp', cb, lmat, xdt)
    decay_to_end = jnp.exp(a_cum[:, :, -1:] - a_cum)
    chunk_states = jnp.einsum('bclgn,bclgh,bclghp->bcghpn', bm, decay_to_end, xdt)
    chunk_decay = jnp.exp(a_cum[:, :, -1])

    def step(state, inp):
        s_c, d_c = inp
        return state * d_c[..., None, None] + s_c, state

    h0 = jnp.zeros((bsz, SSD_GROUPS, hpg, SSD_HEAD_DIM, SSD_STATE), f32)
    _, h_in = lax.scan(step, h0, (jnp.moveaxis(chunk_states, 1, 0), jnp.moveaxis(chunk_decay, 1, 0)))
    h_in = jnp.moveaxis(h_in, 0, 1)
    y_off = jnp.einsum('bclgn,bcghpn,bclgh->bclghp', cm, h_in, jnp.exp(a_cum))
    y = y_diag + y_off + xs * d_skip.astype(f32).reshape(SSD_GROUPS, hpg)[..., None]
    y = y.reshape(bsz, s_len, SSD_WIDTH) * jax.nn.silu(z.astype(f32))
    y = rms_norm(y.reshape(bsz, s_len, SSD_GROUPS, SSD_WIDTH // SSD_GROUPS),
                 norm_g.reshape(SSD_GROUPS, SSD_WIDTH // SSD_GROUPS))
    return y.reshape(bsz, s_len, SSD_WIDTH).astype(cols.dtype)


def mla_mixer(cols, cos, sin, q_norm_g, w_uq, kv_norm_g, w_ukv):
    bsz, s_len, _ = cols.shape
    c_q, c_kv, k_rope = jnp.split(cols, [MLA_Q_LORA, MLA_Q_LORA + MLA_KV_LORA], axis=-1)
    q = (rms_norm(c_q, q_norm_g) @ w_uq).reshape(bsz, s_len, MLA_HEADS, MLA_NOPE + MLA_ROPE)
    q_nope, q_rope = jnp.split(q, [MLA_NOPE], axis=-1)
    q_rope = apply_rope(q_rope, cos[:, :, None], sin[:, :, None])
    k_rope = apply_rope(k_rope, cos, sin)
    kv = (rms_norm(c_kv, kv_norm_g) @ w_ukv).reshape(bsz, s_len, MLA_HEADS, MLA_NOPE + MLA_V)
    k_nope, v = jnp.split(kv, [MLA_NOPE], axis=-1)
    scale = (MLA_NOPE + MLA_ROPE) ** -0.5
    outs = []
    for qb in range(s_len // Q_BLOCK):
        q0 = qb * Q_BLOCK
        kend = q0 + Q_BLOCK
        s = (jnp.einsum('bqhd,bkhd->bhqk', q_nope[:, q0:kend], k_nope[:, :kend])
             + jnp.einsum('bqhr,bkr->bhqk', q_rope[:, q0:kend], k_rope[:, :kend]))
        s = s.astype(jnp.float32) * scale
        q_chunk = (q0 + jnp.arange(Q_BLOCK)) // CHUNK
        k_chunk = jnp.arange(kend) // CHUNK
        s = jnp.where(k_chunk[None, :] <= q_chunk[:, None], s, -jnp.inf)
        p = jax.nn.softmax(s, axis=-1).astype(v.dtype)
        outs.append(jnp.einsum('bhqk,bkhd->bqhd', p, v[:, :kend]))
    return jnp.concatenate(outs, axis=1).reshape(bsz, s_len, MLA_WIDTH)


def gdn_mixer(cols, conv_w, dt_bias, a_log, norm_g):
    f32 = jnp.float32
    bsz, s_len, _ = cols.shape
    nc = s_len // CHUNK
    qkv, z, b_raw, a_raw = jnp.split(
        cols, [GDN_CONV_DIM, GDN_CONV_DIM + GDN_WIDTH, GDN_CONV_DIM + GDN_WIDTH + GDN_HEADS], axis=-1)
    qkv = jax.nn.silu(causal_depthwise_conv(qkv, conv_w)).astype(f32)
    q, k, v = jnp.split(qkv, [GDN_QK_WIDTH, 2 * GDN_QK_WIDTH], axis=-1)

    def to_chunks(t, d):
        return t.reshape(bsz, nc, CHUNK, GDN_HEADS, d).transpose(0, 1, 3, 2, 4)

    def heads_to_chunks(t):
        return t.reshape(bsz, nc, CHUNK, GDN_HEADS).transpose(0, 1, 3, 2)

    q = l2_normalize(to_chunks(q, GDN_DK)) * GDN_DK ** -0.5
    k = l2_normalize(to_chunks(k, GDN_DK))
    v = to_chunks(v, GDN_DV)
    beta = heads_to_chunks(jax.nn.sigmoid(b_raw.astype(f32)))
    g = -jnp.exp(a_log.astype(f32)) * jax.nn.softplus(a_raw.astype(f32) + dt_bias.astype(f32))
    g_cum = jnp.cumsum(heads_to_chunks(g), axis=-1)
    incl = jnp.tril(jnp.ones((CHUNK, CHUNK), bool))
    strict = jnp.tril(jnp.ones((CHUNK, CHUNK), bool), -1)
    gamma = jnp.exp(jnp.where(incl, g_cum[..., :, None] - g_cum[..., None, :], -jnp.inf))
    kb = k * beta[..., None]
    m = jnp.where(strict, jnp.einsum('bchld,bchsd->bchls', kb, k) * gamma, 0.0)
    eye = jnp.eye(CHUNK, dtype=f32)
    rhs = jnp.concatenate([v * beta[..., None], kb * jnp.exp(g_cum)[..., None]], axis=-1)
    sol = lax.linalg.triangular_solve(eye + m, rhs, left_side=True, lower=True, unit_diagonal=True)
    u, w = jnp.split(sol, [GDN_DV], axis=-1)
    qk = jnp.where(incl, jnp.einsum('bchld,bchsd->bchls', q, k) * gamma, 0.0)
    q_dec = q * jnp.exp(g_cum)[..., None]
    k_dec = k * jnp.exp(g_cum[..., -1:] - g_cum)[..., None]
    chunk_decay = jnp.exp(g_cum[..., -1])

    def step(state, inp):
        u_c, w_c, qk_c, qd_c, kd_c, dec_c = inp
        v_new = u_c - jnp.einsum('bhld,bhdv->bhlv', w_c, state)
        o_c = jnp.einsum('bhld,bhdv->bhlv', qd_c, state) + jnp.einsum('bhls,bhsv->bhlv', qk_c, v_new)
        state = state * dec_c[..., None, None] + jnp.einsum('bhld,bhlv->bhdv', kd_c, v_new)
        return state, o_c

    s0 = jnp.zeros((bsz, GDN_HEADS, GDN_DK, GDN_DV), f32)
    xs = (jnp.moveaxis(u, 1, 0), jnp.moveaxis(w, 1, 0), jnp.moveaxis(qk, 1, 0),
          jnp.moveaxis(q_dec, 1, 0), jnp.moveaxis(k_dec, 1, 0), jnp.moveaxis(chunk_decay, 1, 0))
    _, o = lax.scan(step, s0, xs)
    o = jnp.moveaxis(o, 0, 1).transpose(0, 1, 3, 2, 4).reshape(bsz, s_len, GDN_HEADS, GDN_DV)
    o = rms_norm(o, norm_g) * jax.nn.silu(z.astype(f32).reshape(bsz, s_len, GDN_HEADS, GDN_DV))
    return o.reshape(bsz, s_len, GDN_WIDTH).astype(cols.dtype)


def memory_cross_attention(u, mem_n, w_q, w_kv, w_o):
    bsz, s_len, _ = u.shape
    n_mem = mem_n.shape[1]
    q = (u @ w_q).reshape(bsz, s_len, XA_HEADS, XA_HEAD_DIM)
    k, v = jnp.split(mem_n @ w_kv, 2, axis=-1)
    k = k.reshape(bsz, n_mem, XA_HEADS, XA_HEAD_DIM)
    v = v.reshape(bsz, n_mem, XA_HEADS, XA_HEAD_DIM)
    s = jnp.einsum('bqhd,bmhd->bhqm', q, k).astype(jnp.float32) * XA_HEAD_DIM ** -0.5
    p = jax.nn.softmax(s, axis=-1).astype(v.dtype)
    o = jnp.einsum('bhqm,bmhd->bqhd', p, v).reshape(bsz, s_len, D_MODEL)
    return o @ w_o


def _dt_bias(k, shape):
    dt = jnp.exp(jax.random.uniform(k, shape, jnp.float32, math.log(1e-3), math.log(1e-1)))
    return dt + jnp.log(-jnp.expm1(-dt))


def setup_inputs(seed: int = 0) -> dict:
    key = jax.random.key(seed)
    ks = jax.random.split(key, 25)
    f32 = jnp.float32

    def dense(k, shape, fan_in):
        return jax.random.normal(k, shape, f32) * fan_in ** -0.5

    def gain(k, shape):
        return 1.0 + 0.02 * jax.random.normal(k, shape, f32)

    x = jax.random.normal(ks[0], (BATCH, SEQ, D_MODEL), f32)
    mem = jax.random.normal(ks[1], (BATCH, N_MEM, D_MODEL), f32)
    offsets = jax.random.randint(ks[2], (BATCH, 1), 0, 4096, dtype=jnp.int32)
    positions = offsets + jnp.arange(SEQ, dtype=jnp.int32)[None, :]
    return {
        'x': x,
        'mem': mem,
        'positions': positions,
        'norm_g': gain(ks[3], (DEPTH, N_NORMS, D_MODEL)),
        'ffn_w_up': dense(ks[4], (DEPTH, 2, D_MODEL, 2 * D_FF), D_MODEL),
        'ffn_w_down': dense(ks[5], (DEPTH, 2, D_FF, D_MODEL), D_FF),
        'w_in': dense(ks[6], (DEPTH, D_MODEL, IN_COLS), D_MODEL),
        'ssd_conv_w': dense(ks[7], (DEPTH, CONV_K, SSD_CONV_DIM), CONV_K),
        'ssd_conv_b': 0.02 * jax.random.normal(ks[8], (DEPTH, SSD_CONV_DIM), f32),
        'ssd_dt_bias': _dt_bias(ks[9], (DEPTH, SSD_HEADS)),
        'ssd_a_log': jnp.log(jax.random.uniform(ks[10], (DEPTH, SSD_HEADS), f32, 1.0, 16.0)),
        'ssd_d': 1.0 + 0.1 * jax.random.normal(ks[11], (DEPTH, SSD_HEADS), f32),
        'ssd_norm_g': gain(ks[12], (DEPTH, SSD_WIDTH)),
        'mla_q_norm_g': gain(ks[13], (DEPTH, MLA_Q_LORA)),
        'mla_w_uq': dense(ks[14], (DEPTH, MLA_Q_LORA, MLA_HEADS * (MLA_NOPE + MLA_ROPE)), MLA_Q_LORA),
        'mla_kv_norm_g': gain(ks[15], (DEPTH, MLA_KV_LORA)),
        'mla_w_ukv': dense(ks[16], (DEPTH, MLA_KV_LORA, MLA_HEADS * (MLA_NOPE + MLA_V)), MLA_KV_LORA),
        'gdn_conv_w': dense(ks[17], (DEPTH, CONV_K, GDN_CONV_DIM), CONV_K),
        'gdn_dt_bias': _dt_bias(ks[18], (DEPTH, GDN_HEADS)),
        'gdn_a_log': jnp.log(jax.random.uniform(ks[19], (DEPTH, GDN_HEADS), f32, 1.0, 16.0)),
        'gdn_norm_g': gain(ks[20], (DEPTH, GDN_DV)),
        'w_out': dense(ks[21], (DEPTH, D_MIX, D_MODEL), D_MIX),
        'xa_w_q': dense(ks[22], (DEPTH, D_MODEL, D_MODEL), D_MODEL),
        'xa_w_kv': dense(ks[23], (DEPTH, D_MODEL, 2 * D_MODEL), D_MODEL),
        'xa_w_o': dense(ks[24], (DEPTH, D_MODEL, D_MODEL), D_MODEL),
    }


def reference(x, mem, positions, norm_g, ffn_w_up, ffn_w_down, w_in, ssd_conv_w, ssd_conv_b,
              ssd_dt_bias, ssd_a_log, ssd_d, ssd_norm_g, mla_q_norm_g, mla_w_uq, mla_kv_norm_g,
              mla_w_ukv, gdn_conv_w, gdn_dt_bias, gdn_a_log, gdn_norm_g, w_out, xa_w_q, xa_w_kv,
              xa_w_o):
    cos, sin = rope_tables(positions)
    h = x
    for i in range(DEPTH):
        g = norm_g[i]
        ff = swiglu_ffn(rms_norm(h, g[FFN1_PRE]), ffn_w_up[i, 0], ffn_w_down[i, 0])
        h = h + 0.5 * rms_norm(ff, g[FFN1_POST])
        u = rms_norm(h, g[MIX_PRE])
        cols = u @ w_in[i]
        c_ssd, c_mla, c_gdn = jnp.split(cols, [SSD_IN, SSD_IN + MLA_IN], axis=-1)
        y_ssd = ssd_mixer(c_ssd, ssd_conv_w[i], ssd_conv_b[i], ssd_dt_bias[i], ssd_a_log[i],
                          ssd_d[i], ssd_norm_g[i])
        y_mla = mla_mixer(c_mla, cos, sin, mla_q_norm_g[i], mla_w_uq[i], mla_kv_norm_g[i], mla_w_ukv[i])
        y_gdn = gdn_mixer(c_gdn, gdn_conv_w[i], gdn_dt_bias[i], gdn_a_log[i], gdn_norm_g[i])
        y = jnp.concatenate([y_ssd, y_mla, y_gdn], axis=-1)
        h = h + rms_norm(y @ w_out[i], g[MIX_POST])
        xa = memory_cross_attention(rms_norm(h, g[XA_PRE]), rms_norm(mem, g[MEM_NORM]),
                                    xa_w_q[i], xa_w_kv[i], xa_w_o[i])
        h = h + rms_norm(xa, g[XA_POST])
        ff = swiglu_ffn(rms_norm(h, g[FFN2_PRE]), ffn_w_up[i, 1], ffn_w_down[i, 1])
        h = h + 0.5 * rms_norm(ff, g[FFN2_POST])
    return h
```

```python
from contextlib import ExitStack
from concourse.bass_utils import run_bass_kernel_spmd
import numpy as np
import concourse.bass as bass
import concourse.mybir as mybir

F32 = mybir.dt.float32
BF16 = mybir.dt.bfloat16
I32 = mybir.dt.int32
AF = mybir.ActivationFunctionType
ALU = mybir.AluOpType
AX = mybir.AxisListType

_DSZ = {F32: 4, BF16: 2, I32: 4, mybir.dt.float32r: 4, mybir.dt.uint32: 4,
        mybir.dt.float16: 2, mybir.dt.uint16: 2, mybir.dt.int16: 2, mybir.dt.uint8: 1, mybir.dt.int8: 1}

FAST_RECIP = True
ENGS = ("pe", "act", "dve", "pool", "sp")
COMPUTE = ("pe", "act", "dve", "pool")


def region(ap):
    sz = _DSZ[ap.dtype]
    pat = list(ap.ap)
    space = str(ap.space)
    if space == "DRAM":
        ext = sum((c - 1) * abs(s) for s, c in pat) + 1
        W = int(ap.tensor.shape[-1])
        c0, c1 = 0, W
        ls, ln = pat[-1]
        if ls == 1 and ln <= W and all(s % W == 0 for s, c in pat[:-1]) and (ap.offset % W) + ln <= W:
            c0 = ap.offset % W
            c1 = c0 + ln
        return (("D", ap.name), 0, 1, ap.offset * sz, (ap.offset + ext) * sz, c0, c1)
    if space == "PSUM":
        return ((space, ap.name), 0, 128, 0, 1 << 30, 0, 1 << 40)
    pstride, pcnt = pat[0]
    if pstride == 0:
        pstride = 1 << 40
    p0 = ap.offset // pstride if pstride < (1 << 40) else 0
    col = ap.offset - p0 * pstride if pstride < (1 << 40) else ap.offset
    ext = sum((c - 1) * abs(s) for s, c in pat[1:]) + 1
    return ((space, ap.name), p0, p0 + pcnt, col * sz, (col + ext) * sz, 0, 1 << 40)


def _ov(a, b):
    return a[1] < b[2] and b[1] < a[2] and a[3] < b[4] and b[3] < a[4] and a[5] < b[6] and b[5] < a[6]


def _cov(a, b):
    return a[1] <= b[1] and a[2] >= b[2] and a[3] <= b[3] and a[4] >= b[4] and a[5] <= b[5] and a[6] >= b[6]


class Op:
    __slots__ = ("eng", "fn", "deps", "dma", "idx", "inc", "tok", "waits")

    def __init__(self, eng, fn, dma):
        self.eng, self.fn, self.dma = eng, fn, dma
        self.deps = []
        self.inc = False
        self.tok = None
        self.waits = []


class Sched:
    def __init__(self, nc, sems, ndma=24):
        self.nc = nc
        self.sems = sems
        self.cnt = {e: 0 for e in COMPUTE}
        self.dma_val = {}
        self.dma_rr = {e: 0 for e in ENGS}
        self.reset()

    def reset(self):
        self.ops = {e: [] for e in ENGS}
        self.wr = {}
        self.rd = {}

    def add(self, eng, fn, reads=(), writes=(), dma=False):
        op = Op(eng, fn, dma)
        op.idx = len(self.ops[eng])
        reads = list(reads)
        writes = list(writes) + [a for a in reads if str(a.space) == "PSUM"]
        deps = []
        for ap in reads:
            r = region(ap)
            for (rr, o) in self.wr.get(r[0], ()):
                if _ov(r, rr):
                    deps.append((o, "raw"))
        wregs = []
        for ap in writes:
            r = region(ap)
            wregs.append(r)
            for (rr, o) in self.wr.get(r[0], ()):
                if _ov(r, rr):
                    deps.append((o, "waw"))
            for (rr, o) in self.rd.get(r[0], ()):
                if _ov(r, rr):
                    deps.append((o, "war"))
        op.deps = deps
        for ap in reads:
            r = region(ap)
            lst = self.rd.setdefault(r[0], [])
            if not dma:
                for i, (rr, o) in enumerate(lst):
                    if rr == r and o.eng == eng and not o.dma:
                        lst[i] = (r, op)
                        break
                else:
                    lst.append((r, op))
            else:
                lst.append((r, op))
        for r in wregs:
            k = r[0]
            self.wr[k] = [(rr, o) for (rr, o) in self.wr.get(k, ()) if not _cov(r, rr)] + [(r, op)]
            if k in self.rd:
                self.rd[k] = [(rr, o) for (rr, o) in self.rd[k] if not _cov(r, rr) or o is op]
        self.ops[eng].append(op)
        return op

    def emit_block(self, wait_all_dma=True):
        nc = self.nc
        self._assign_tokens()
        val = self._flag_and_count()
        self._compute_waits(val)
        engmap = {"pe": "tensor", "act": "scalar", "dve": "vector", "pool": "gpsimd", "sp": "sync"}
        with nc.Block() as block:
            for e in ENGS:
                ops = self.ops[e]
                tail = []
                if wait_all_dma:
                    for s in self.sems["dma"][e]:
                        v = self.dma_val.get(id(s), 0)
                        if v > 0:
                            tail.append((s, v))
                if not ops and not tail:
                    continue

                def body(eng, ops=ops, e=e, tail=tail):
                    for op in ops:
                        for (s, v) in op.waits:
                            eng.wait_ge(s, v)
                        inst = op.fn(eng)
                        if op.dma:
                            inst.then_inc(op.tok[0], 16)
                        elif op.inc:
                            inst.then_inc(self.sems[e], 1)
                    for (s, v) in tail:
                        eng.wait_ge(s, v)

                getattr(block, engmap[e])(body)
        self.reset()

    def _assign_tokens(self):
        self._reuse_wait = {}
        for e in ENGS:
            for op in self.ops[e]:
                if op.dma:
                    pool = self.sems["dma"][e]
                    s = pool[self.dma_rr[e] % len(pool)]
                    self.dma_rr[e] += 1
                    prev = self.dma_val.get(id(s), 0)
                    self.dma_val[id(s)] = prev + 16
                    op.tok = (s, prev + 16)
                    self._reuse_wait[id(op)] = (s, prev)

    def _needs_edge(self, o, op, kind):
        if o.dma:
            return True
        if o.eng == op.eng and not op.dma:
            if o.eng == "pe" or kind != "raw":
                return False
        return True

    def _flag_and_count(self):
        for e in ENGS:
            for op in self.ops[e]:
                for (o, kind) in op.deps:
                    if not o.dma and self._needs_edge(o, op, kind):
                        o.inc = True
        val = {}
        for e in COMPUTE:
            c = self.cnt[e]
            for op in self.ops[e]:
                if not op.dma and op.inc:
                    c += 1
                    val[id(op)] = c
            self.cnt[e] = c
        return val

    def _compute_waits(self, val):
        for e in ENGS:
            waited = {}
            for op in self.ops[e]:
                need = {}
                if op.dma:
                    s, prev = self._reuse_wait[id(op)]
                    if prev > 0:
                        need[id(s)] = (s, prev)
                for (o, kind) in op.deps:
                    if not self._needs_edge(o, op, kind):
                        continue
                    if o.dma:
                        s, v = o.tok
                    else:
                        s, v = self.sems[o.eng], val[id(o)]
                    if need.get(id(s), (None, 0))[1] < v:
                        need[id(s)] = (s, v)
                op.waits = []
                for k, (s, v) in need.items():
                    if waited.get(k, 0) < v:
                        op.waits.append((s, v))
                        waited[k] = v


class Ctx:
    def __init__(self, nc, S):
        self.nc = nc
        self.S = S
        self.stack = ExitStack()
        sems = {}
        for e in COMPUTE:
            sems[e] = self.stack.enter_context(nc.semaphore("c_" + e))
        sems["dma"] = {}
        for e, n in (("sp", 24), ("pool", 12), ("act", 8), ("dve", 2), ("pe", 2)):
            sems["dma"][e] = [self.stack.enter_context(nc.semaphore("d_%s%d" % (e, i))) for i in range(n)]
        self.sch = Sched(nc, sems)
        self.uid = 0

    def phase(self):
        return Phase(self)


class Phase:
    def __init__(self, ctx):
        self.c = ctx
        self.nc = ctx.nc
        self.s = ctx.sch
        self.st = ExitStack()

    def __enter__(self):
        self.st.__enter__()
        return self

    def __exit__(self, *a):
        if a[0] is None:
            self.s.emit_block()
        return self.st.__exit__(*a)

    def sb(self, name, shape, dt=F32):
        self.c.uid += 1
        return self.st.enter_context(self.nc.sbuf_tensor("%s_%d" % (name, self.c.uid), list(shape), dt))

    def ps(self, name, shape=(128, 512), dt=F32):
        self.c.uid += 1
        return self.st.enter_context(self.nc.psum_tensor("%s_%d" % (name, self.c.uid), list(shape), dt))

    def mm(self, out, lhsT, rhs, start=True, stop=True):
        self.s.add("pe", lambda e: e.matmul(out, lhsT, rhs, start=start, stop=stop),
                   reads=[lhsT, rhs], writes=[out])

    def tr(self, out, in_, ident):
        self.s.add("pe", lambda e: e.transpose(out, in_, ident), reads=[in_, ident], writes=[out])

    def act(self, out, in_, func, bias=None, scale=None, accum=None, eng="act"):
        kw = {}
        rd = [in_]
        wr = [out]
        if bias is not None:
            kw["bias"] = bias
            if not isinstance(bias, (int, float)):
                rd.append(bias)
        if scale is not None:
            kw["scale"] = scale
            if not isinstance(scale, (int, float)):
                rd.append(scale)
        if accum is not None:
            kw["accum_out"] = accum
            wr.append(accum)
        self.s.add("act", lambda e: e.activation(out, in_, func, **kw), reads=rd, writes=wr)

    def tt(self, eng, out, in0, in1, op):
        self.s.add(eng, lambda e: e.tensor_tensor(out, in0, in1, op), reads=[in0, in1], writes=[out])

    def ts(self, eng, out, in0, s1, op0, s2=None, op1=None, accum=None):
        rd = [in0] + [x for x in (s1, s2) if x is not None and not isinstance(x, (int, float))]
        wr = [out] + ([accum] if accum is not None else [])
        kw = {}
        if op1 is not None:
            kw["op1"] = op1
        if accum is not None:
            kw["accum_out"] = accum
        self.s.add(eng, lambda e: e.tensor_scalar(out, in0, s1, s2, op0, **kw), reads=rd, writes=wr)

    def stt(self, out, in0, scalar, in1, op0, op1, eng="dve"):
        rd = [in0, in1] + ([scalar] if not isinstance(scalar, (int, float)) else [])
        self.s.add(eng, lambda e: e.scalar_tensor_tensor(out, in0, scalar, in1, op0, op1), reads=rd, writes=[out])

    def copy(self, eng, out, in_):
        if eng == "act":
            self.s.add("act", lambda e: e.copy(out, in_), reads=[in_], writes=[out])
        else:
            self.s.add(eng, lambda e: e.tensor_copy(out, in_), reads=[in_], writes=[out])

    def recip(self, out, in_):
        if int(np.prod(out.shape[1:])) >= 64:
            self.act(out, in_, AF.Ln)
            self.act(out, out, AF.Exp, scale=-1.0)
        else:
            self.s.add("dve", lambda e: e.reciprocal(out, in_), reads=[in_], writes=[out])

    def memset(self, eng, ap, v):
        self.s.add(eng, lambda e: e.memset(ap, v), reads=[], writes=[ap])

    def dma(self, eng, out, in_):
        self.s.add(eng, lambda e: e.dma_start(out=out, in_=in_), reads=[in_], writes=[out], dma=True)

    def rstd(self, out, ss, inv_n, eps=1e-6):
        if int(np.prod(out.shape[1:])) >= 64:
            self.act(out, ss, AF.Ln, scale=float(inv_n), bias=self.epsb(eps))
            self.act(out, out, AF.Exp, scale=-0.5)
        else:
            self.ts("dve", out, ss, inv_n, ALU.mult, eps, ALU.add)
            self.act(out, out, AF.Sqrt)
            self.s.add("dve", lambda e: e.reciprocal(out, out), reads=[out], writes=[out])

    def epsb(self, eps):
        if getattr(self, "_epsb", None) is None:
            self._epsb = self.sb("epsb", [128, 1])
            self.memset("dve", self._epsb[:], float(eps))
        return self._epsb[:]

D = 1024
DEPTH = 4
D_FF = 2816
N_MEM = 256
IN_COLS = 2736
FFN1_PRE, FFN1_POST, MIX_PRE, MIX_POST, MEM_NORM, XA_PRE, XA_POST, FFN2_PRE, FFN2_POST = range(9)

PP_NORM = 0
PP_SCW = 72
PP_SCB = 96
PP_GCW = 102
PP_QG = 126
PP_KVG = 128
PP_SDTB = 129
PP_SALOG = 137
PP_SD = 145
PP_GDTB = 153
PP_GALOG = 157
PP_SNG = 161
PP_GNG = 673
NPP = 737


DEBUG = {}


def ffn_phase(ctx, W, L, which, HT_in, HT_out):
    S = ctx.S
    TS = min(1024, S)
    NSUP = S // TS
    NSUB = TS // 512
    w_up = W["ffn_w_up"][L, which]
    w_dn = W["ffn_w_down"][L, which]
    pre = (FFN1_PRE, FFN2_PRE)[which]
    post = (FFN1_POST, FFN2_POST)[which]
    HTi = HT_in.rearrange("(c p) t -> p c t", p=128)
    HTo = HT_out.rearrange("(c p) t -> p c t", p=128)
    with ctx.phase() as p:
        pp = p.sb("pp", [128, NPP])
        p.dma("sp", pp[:], W["pp"][L])
        ones = p.sb("ones", [128, 128], BF16)
        p.memset("dve", ones[:], 1.0)
        ghalf = p.sb("gh", [128, 8])
        p.ts("dve", ghalf[:], pp[:, post * 8:(post + 1) * 8], 0.5, ALU.mult)
        wdn = p.sb("wdn", [128, 22, 1024], BF16)
        hbuf = [p.sb("hb%d" % i, [128, 8, 512]) for i in range(2)]
        uTs = [p.sb("uT%d" % i, [128, 8, TS], BF16) for i in range(2)]
        hid = p.sb("hid", [128, 22, TS], BF16)
        wup = [p.sb("wup%d" % i, [128, 8, 512], BF16) for i in range(2)]
        ff = p.sb("ff", [128, 8, 512])
        sq = p.sb("sq", [128, 8, 512], BF16)
        rbc = p.sb("rbc", [128, 512])
        sl = [p.sb("sl%d" % i, [128, 512], BF16) for i in range(2)]
        pg = [p.ps("pg%d" % i) for i in range(2)]
        pu = [p.ps("pu%d" % i) for i in range(2)]
        pss = p.ps("pss")
        pd = [p.ps("pd%d" % i) for i in range(2)]
        st = {"hb": 0}

        def next_hb():
            hb = hbuf[st["hb"] % 2]
            st["hb"] += 1
            return hb

        def stage_a(su):
            uT = uTs[su % 2]
            for s_ in range(NSUB):
                c0 = su * TS + s_ * 512
                hb = next_hb()
                p.dma("sp", hb[:], HTi[:, :, c0:c0 + 512])
                prenorm(p, hb, sq, ones, pss, rbc, pp[:, pre * 8:pre * 8 + 8],
                        lambda kc, uT=uT, s_=s_: uT[:, kc, s_ * 512:(s_ + 1) * 512])

        def stage_b(su):
            uT = uTs[su % 2]
            for jp in range(11):
                wb = wup[jp % 2]
                p.dma("pool", wb[:, :, 0:256],
                      w_up[:, jp * 256:(jp + 1) * 256].rearrange("(c p) f -> p c f", p=128))
                p.dma("pool", wb[:, :, 256:512],
                      w_up[:, D_FF + jp * 256:D_FF + (jp + 1) * 256].rearrange("(c p) f -> p c f", p=128))
                if su == 0 and jp == 1:
                    for half in range(2):
                        p.dma("pool", wdn[:, half * 11:(half + 1) * 11, :],
                              w_dn[half * 1408:(half + 1) * 1408, :].rearrange("(c p) f -> p c f", p=128))
                for jj in range(2):
                    j = jp * 2 + jj
                    for s_ in range(NSUB):
                        cs = slice(s_ * 512, (s_ + 1) * 512)
                        gps, ups = pg[s_ % 2], pu[s_ % 2]
                        for kc in range(8):
                            p.mm(gps[:], wb[:, kc, jj * 128:(jj + 1) * 128], uT[:, kc, cs], start=(kc == 0), stop=(kc == 7))
                        for kc in range(8):
                            p.mm(ups[:], wb[:, kc, 256 + jj * 128:256 + (jj + 1) * 128], uT[:, kc, cs], start=(kc == 0), stop=(kc == 7))
                        p.act(sl[s_ % 2][:], gps[:], AF.Silu)
                        p.tt("dve", hid[:, j, cs], sl[s_ % 2][:], ups[:], ALU.mult)

        def stage_c(su):
            for s_ in range(NSUB):
                c0 = su * TS + s_ * 512
                cs = slice(s_ * 512, (s_ + 1) * 512)
                hb = next_hb()
                p.dma("sp", hb[:], HTi[:, :, c0:c0 + 512])
                proj_post(p, wdn, 22, lambda j, cs=cs: hid[:, j, cs], pd, ff, sq, ones, pss, rbc, ghalf, hb)
                p.dma("sp", HTo[:, :, c0:c0 + 512], hb[:])

        stage_a(0)
        for su in range(NSUP):
            stage_b(su)
            if su + 1 < NSUP:
                stage_a(su + 1)
            stage_c(su)


def to_feature_major(ctx, W, x, HT):
    S = ctx.S
    HTv = HT.rearrange("(c p) t -> p c t", p=128)
    with ctx.phase() as p:
        ident = p.sb("ident", [128, 128])
        p.dma("sp", ident[:], W["ident"])
        xin = [p.sb("xin%d" % i, [128, 1024]) for i in range(2)]
        xo = [p.sb("xo%d" % i, [128, 8, 512]) for i in range(2)]
        pt = [p.ps("pt%d" % i) for i in range(4)]
        for g in range(S // 512):
            o = xo[g % 2]
            for tt in range(4):
                t0 = g * 512 + tt * 128
                xi = xin[tt % 2]
                p.dma("sp", xi[:], x[t0:t0 + 128, :])
                for hf in range(2):
                    ptt = pt[(tt % 2) * 2 + hf]
                    for q in range(4):
                        kc = hf * 4 + q
                        p.tr(ptt[:, q * 128:(q + 1) * 128], xi[:, kc * 128:(kc + 1) * 128], ident[:])
                    dst = o[:, hf * 4:(hf + 1) * 4, tt * 128:(tt + 1) * 128]
                    p.copy("dve" if hf == 0 else "act", dst, ptt[:].rearrange("p (q t) -> p q t", q=4))
            p.dma("sp", HTv[:, :, g * 512:(g + 1) * 512], o[:])


def to_token_major(ctx, W, HT, y):
    S = ctx.S
    HTv = HT.rearrange("(c p) t -> p c t", p=128)
    with ctx.phase() as p:
        ident = p.sb("ident", [128, 128])
        p.dma("sp", ident[:], W["ident"])
        hin = [p.sb("hin%d" % i, [128, 8, 512]) for i in range(2)]
        yo = [p.sb("yo%d" % i, [128, 1024]) for i in range(2)]
        pt = [p.ps("pt%d" % i) for i in range(4)]
        for g in range(S // 512):
            hi = hin[g % 2]
            p.dma("sp", hi[:], HTv[:, :, g * 512:(g + 1) * 512])
            for tt in range(4):
                t0 = g * 512 + tt * 128
                o = yo[tt % 2]
                for hf in range(2):
                    ptt = pt[(tt % 2) * 2 + hf]
                    for q in range(4):
                        kc = hf * 4 + q
                        p.tr(ptt[:, q * 128:(q + 1) * 128], hi[:, kc, tt * 128:(tt + 1) * 128], ident[:])
                    p.copy("dve" if hf == 0 else "act", o[:, hf * 512:(hf + 1) * 512], ptt[:])
                p.dma("sp", y[t0:t0 + 128, :], o[:])


def make_pp(inp, depth):
    pp = np.zeros((depth, 128, NPP), np.float32)
    for l in range(depth):
        ng = np.asarray(inp["norm_g"][l], np.float32)
        pp[l, :, PP_NORM:PP_NORM + 72] = ng.reshape(9, 8, 128).transpose(2, 0, 1).reshape(128, 72)
        cw = np.asarray(inp["ssd_conv_w"][l], np.float32)
        pp[l, :, PP_SCW:PP_SCW + 24] = cw.reshape(4, 6, 128).transpose(2, 1, 0).reshape(128, 24)
        pp[l, :, PP_SCB:PP_SCB + 6] = np.asarray(inp["ssd_conv_b"][l], np.float32).reshape(6, 128).T
        gw = np.asarray(inp["gdn_conv_w"][l], np.float32)
        pp[l, :, PP_GCW:PP_GCW + 24] = gw.reshape(4, 6, 128).transpose(2, 1, 0).reshape(128, 24)
        pp[l, :, PP_QG:PP_QG + 2] = np.asarray(inp["mla_q_norm_g"][l], np.float32).reshape(2, 128).T
        pp[l, :, PP_KVG:PP_KVG + 1] = np.asarray(inp["mla_kv_norm_g"][l], np.float32).reshape(1, 128).T
        for off, key, n in ((PP_SDTB, "ssd_dt_bias", 8), (PP_SALOG, "ssd_a_log", 8), (PP_SD, "ssd_d", 8),
                            (PP_GDTB, "gdn_dt_bias", 4), (PP_GALOG, "gdn_a_log", 4),
                            (PP_SNG, "ssd_norm_g", 512), (PP_GNG, "gdn_norm_g", 64)):
            pp[l, :, off:off + n] = np.broadcast_to(np.asarray(inp[key][l], np.float32)[None, :], (128, n))
    return pp


def prenorm(p, hb, sq, ones, pss, rbc, gcols, dst):
    p.act(sq[:], hb[:], AF.Square)
    for kc in range(8):
        p.mm(pss[:], ones[:], sq[:, kc, :], start=(kc == 0), stop=(kc == 7))
    p.rstd(rbc[:], pss[:], 1.0 / D)
    for kc in range(8):
        p.stt(dst(kc), hb[:, kc, :], gcols[:, kc:kc + 1], rbc[:], ALU.mult, ALU.mult)


def proj_post(p, wt, KC, rhs, pd, ff, sq, ones, pss, rbc, gpost, hb):
    for fo in range(8):
        pdd = pd[fo % 2]
        for kc in range(KC):
            p.mm(pdd[:], wt[:, kc, fo * 128:(fo + 1) * 128], rhs(kc), start=(kc == 0), stop=(kc == KC - 1))
        p.copy("dve", ff[:, fo, :], pdd[:])
        p.act(sq[:, fo, :], ff[:, fo, :], AF.Square)
    for fo in range(8):
        p.mm(pss[:], ones[:], sq[:, fo, :], start=(fo == 0), stop=(fo == 7))
    p.rstd(rbc[:], pss[:], 1.0 / D)
    for kc in range(8):
        p.stt(ff[:, kc, :], ff[:, kc, :], gpost[:, kc:kc + 1], rbc[:], ALU.mult, ALU.mult)
        p.tt("dve", hb[:, kc, :], hb[:, kc, :], ff[:, kc, :], ALU.add)


def load_w_bf16(p, dst, src, kc_n, ncols, colchunk=1024):
    v = src.rearrange("(c p) f -> p c f", p=128)
    for c0 in range(0, ncols, colchunk):
        c1 = min(ncols, c0 + colchunk)
        p.dma("pool", dst[:, :, c0:c1], v[:, :, c0:c1])


CF_XBC, CF_QKV, CF_CQ, CF_CKV, CF_KR, CF_ROWS = 0, 768, 1536, 1792, 1920, 1952
CT_ZS, CT_ZG, CT_COLS = 0, 512, 768
CS_B, CS_A, CS_DT, CS_COLS = 0, 4, 8, 16


def inproj_phase(ctx, W, L, HT, COLF, COLT, COLS):
    S = ctx.S
    HTv = HT.rearrange("(c p) t -> p c t", p=128)
    w_in = W["w_in"][L]
    fm = [(512 + i * 128, 128, CF_XBC + i * 128) for i in range(6)] + \
         [(1704 + i * 128, 128, CF_QKV + i * 128) for i in range(6)] + \
         [(1288, 128, CF_CQ), (1416, 128, CF_CQ + 128), (1544, 128, CF_CKV), (1672, 32, CF_KR)]
    with ctx.phase() as p:
        pp = p.sb("pp", [128, NPP])
        p.dma("sp", pp[:], W["pp"][L])
        ones = p.sb("ones", [128, 128], BF16)
        p.memset("dve", ones[:], 1.0)
        wi = p.sb("wi", [128, 8, IN_COLS], BF16)
        load_w_bf16(p, wi, w_in, 8, IN_COLS, 1368)
        hbuf = [p.sb("hb%d" % i, [128, 8, 512]) for i in range(2)]
        uT = p.sb("uT", [128, 8, 512], BF16)
        sq = p.sb("sq", [128, 8, 512], BF16)
        rbc = p.sb("rbc", [128, 512])
        cf = [p.sb("cf%d" % i, [128, 16, 512]) for i in range(2)]
        ct = [p.sb("ct%d" % i, [128, 784]) for i in range(2)]
        pss = p.ps("pss")
        pf = [p.ps("pf%d" % i) for i in range(3)]
        pt = [p.ps("pt%d" % i) for i in range(4)]
        for s_ in range(S // 512):
            c0 = s_ * 512
            hb = hbuf[s_ % 2]
            p.dma("sp", hb[:], HTv[:, :, c0:c0 + 512])
            prenorm(p, hb, sq, ones, pss, rbc, pp[:, PP_NORM + MIX_PRE * 8:PP_NORM + MIX_PRE * 8 + 8],
                    lambda kc: uT[:, kc, :])
            cfb = cf[s_ % 2]
            for i, (wc, m, row) in enumerate(fm):
                ps_ = pf[i % 3]
                for kc in range(8):
                    p.mm(ps_[0:m, :], wi[:, kc, wc:wc + m], uT[:, kc, :], start=(kc == 0), stop=(kc == 7))
                p.copy("act" if i % 2 == 0 else "dve", cfb[0:m, i, :], ps_[0:m, :])
            p.dma("sp", COLF[0:1920, c0:c0 + 512].rearrange("(c p) t -> p c t", p=128), cfb[:, 0:15, :])
            p.dma("sp", COLF[1920:1952, c0:c0 + 512], cfb[0:32, 15, :])
            for tt in range(4):
                ts_ = slice(tt * 128, (tt + 1) * 128)
                pa, pb = pt[(tt % 2) * 2], pt[(tt % 2) * 2 + 1]
                ctb = ct[tt % 2]
                for kc in range(8):
                    p.mm(pa[:], uT[:, kc, ts_], wi[:, kc, 0:512], start=(kc == 0), stop=(kc == 7))
                for kc in range(8):
                    p.mm(pb[:, 0:264], uT[:, kc, ts_], wi[:, kc, 2472:2736], start=(kc == 0), stop=(kc == 7))
                for kc in range(8):
                    p.mm(pb[:, 264:272], uT[:, kc, ts_], wi[:, kc, 1280:1288], start=(kc == 0), stop=(kc == 7))
                p.copy("act", ctb[:, 0:512], pa[:])
                p.copy("dve", ctb[:, 512:784], pb[:, 0:272])
                p.dma("sp", COLT[c0 + tt * 128:c0 + (tt + 1) * 128, :], ctb[:, 0:768])
                p.dma("sp", COLS[c0 + tt * 128:c0 + (tt + 1) * 128, :], ctb[:, 768:784])


def outproj_phase(ctx, W, L, HT, YT):
    S = ctx.S
    HTv = HT.rearrange("(c p) t -> p c t", p=128)
    YTv = YT.rearrange("(c p) t -> p c t", p=128)
    with ctx.phase() as p:
        pp = p.sb("pp", [128, NPP])
        p.dma("sp", pp[:], W["pp"][L])
        ones = p.sb("ones", [128, 128], BF16)
        p.memset("dve", ones[:], 1.0)
        wo = p.sb("wo", [128, 8, 1024], BF16)
        load_w_bf16(p, wo, W["w_out"][L], 8, 1024)
        hbuf = [p.sb("hb%d" % i, [128, 8, 512]) for i in range(2)]
        yb = [p.sb("yb%d" % i, [128, 8, 512], BF16) for i in range(2)]
        ff = p.sb("ff", [128, 8, 512])
        sq = p.sb("sq", [128, 8, 512], BF16)
        rbc = p.sb("rbc", [128, 512])
        pss = p.ps("pss")
        pd = [p.ps("pd%d" % i) for i in range(2)]
        gpost = pp[:, PP_NORM + MIX_POST * 8:PP_NORM + MIX_POST * 8 + 8]
        for s_ in range(S // 512):
            c0 = s_ * 512
            hb, y = hbuf[s_ % 2], yb[s_ % 2]
            p.dma("sp", hb[:], HTv[:, :, c0:c0 + 512])
            p.dma("sp", y[:], YTv[:, :, c0:c0 + 512])
            proj_post(p, wo, 8, lambda kc: y[:, kc, :], pd, ff, sq, ones, pss, rbc, gpost, hb)
            p.dma("sp", HTv[:, :, c0:c0 + 512], hb[:])


def xattn_phase(ctx, W, L, HT, mem):
    S = ctx.S
    HTv = HT.rearrange("(c p) t -> p c t", p=128)
    with ctx.phase() as p:
        pp = p.sb("pp", [128, NPP])
        p.dma("sp", pp[:], W["pp"][L])
        ident = p.sb("ident", [128, 128])
        p.dma("sp", ident[:], W["ident"])
        ones = p.sb("ones", [128, 128], BF16)
        p.memset("dve", ones[:], 1.0)
        wkv = p.sb("wkv", [128, 8, 2048], BF16)
        load_w_bf16(p, wkv, W["xa_w_kv"][L], 8, 2048)
        wq = p.sb("wq", [128, 8, 1024], BF16)
        load_w_bf16(p, wq, W["xa_w_q"][L], 8, 1024)
        wo = p.sb("wo", [128, 8, 1024], BF16)
        load_w_bf16(p, wo, W["xa_w_o"][L], 8, 1024)
        memT = p.sb("memT", [128, 8, 256], BF16)
        kT = p.sb("kT", [128, 8, 256], BF16)
        v = p.sb("v", [128, 2, 1024], BF16)
        hbuf = [p.sb("hb%d" % i, [128, 8, 512]) for i in range(2)]
        uT = p.sb("uT", [128, 8, 512], BF16)
        qT = p.sb("qT", [128, 8, 512], BF16)
        oT = p.sb("oT", [128, 8, 512], BF16)
        pT = [p.sb("pT%d" % i, [128, 512], BF16) for i in range(4)]
        rden = p.sb("rden", [128, 512])
        ff = p.sb("ff", [128, 8, 512])
        sq = p.sb("sq", [128, 8, 512], BF16)
        rbc = p.sb("rbc", [128, 512])
        mt_ = p.sb("mt", [128, 1024])
        st = p.sb("st", [128, 4])
        junk = p.sb("junk", [128, 1024])
        pss = p.ps("pss")
        pd = [p.ps("pd%d" % i) for i in range(2)]
        pq = [p.ps("pq%d" % i) for i in range(2)]
        pden = p.ps("pden")
        po = [p.ps("po%d" % i) for i in range(2)]
        gmem = pp[:, PP_NORM + MEM_NORM * 8:PP_NORM + MEM_NORM * 8 + 8]
        XM = DEBUG.get('XM', 9)
        for mt in range(2):
            p.dma("sp", mt_[:], mem[mt * 128:(mt + 1) * 128, :])
            if XM < 1:
                continue
            p.act(junk[:], mt_[:], AF.Square, accum=st[:, 0:1])
            p.rstd(st[:, 1:2], st[:, 0:1], 1.0 / D)
            p.ts("dve", mt_[:], mt_[:], st[:, 1:2], ALU.mult)
            for hf in range(2):
                if XM < 2:
                    break
                ptt = pq[hf]
                for q in range(4):
                    kc = hf * 4 + q
                    p.tr(ptt[:, q * 128:(q + 1) * 128], mt_[:, kc * 128:(kc + 1) * 128], ident[:])
                for q in range(4):
                    if XM < 3:
                        break
                    kc = hf * 4 + q
                    p.ts("dve", memT[:, kc, mt * 128:(mt + 1) * 128], ptt[:, q * 128:(q + 1) * 128],
                         gmem[:, kc:kc + 1], ALU.mult)
        for dc in range(8):
            if XM < 4:
                break
            ps_ = pq[dc % 2]
            for kc in range(8):
                p.mm(ps_[:, 0:256], wkv[:, kc, dc * 128:(dc + 1) * 128], memT[:, kc, :],
                     start=(kc == 0), stop=(kc == 7))
            p.copy("act" if dc % 2 == 0 else "dve", kT[:, dc, :], ps_[:, 0:256])
        for mt in range(2):
            if XM < 5:
                break
            for hf in range(2):
                ps_ = po[hf]
                for kc in range(8):
                    p.mm(ps_[:], memT[:, kc, mt * 128:(mt + 1) * 128], wkv[:, kc, 1024 + hf * 512:1024 + (hf + 1) * 512],
                         start=(kc == 0), stop=(kc == 7))
                p.copy("act" if hf == 0 else "dve", v[:, mt, hf * 512:(hf + 1) * 512], ps_[:])
        gpre = pp[:, PP_NORM + XA_PRE * 8:PP_NORM + XA_PRE * 8 + 8]
        gpost = pp[:, PP_NORM + XA_POST * 8:PP_NORM + XA_POST * 8 + 8]
        for s_ in range(S // 512):
            if DEBUG.get('XA', 9) < 1:
                break
            c0 = s_ * 512
            hb = hbuf[s_ % 2]
            p.dma("sp", hb[:], HTv[:, :, c0:c0 + 512])
            prenorm(p, hb, sq, ones, pss, rbc, gpre, lambda kc: uT[:, kc, :])
            for dc in range(8):
                ps_ = pq[dc % 2]
                for kc in range(8):
                    p.mm(ps_[:], wq[:, kc, dc * 128:(dc + 1) * 128], uT[:, kc, :], start=(kc == 0), stop=(kc == 7))
                p.act(qT[:, dc, :], ps_[:], AF.Copy, scale=1.0 / 16.0)
            def xa_sc(h):
                for mt in range(2):
                    ps_ = pq[mt]
                    for dc in range(2):
                        p.mm(ps_[:], kT[:, 2 * h + dc, mt * 128:(mt + 1) * 128], qT[:, 2 * h + dc, :],
                             start=(dc == 0), stop=(dc == 1))
                    p.act(pT[(h % 2) * 2 + mt][:], ps_[:], AF.Exp)

            def xa_av(h):
                for mt in range(2):
                    p.mm(pden[:], ones[:], pT[(h % 2) * 2 + mt][:], start=(mt == 0), stop=(mt == 1))
                p.recip(rden[:], pden[:])
                for dc in range(2):
                    ps_ = po[dc]
                    for mt in range(2):
                        p.mm(ps_[:], v[:, mt, (2 * h + dc) * 128:(2 * h + dc + 1) * 128], pT[(h % 2) * 2 + mt][:],
                             start=(mt == 0), stop=(mt == 1))
                    p.tt("dve", oT[:, 2 * h + dc, :], ps_[:], rden[:], ALU.mult)

            xa_sc(0)
            for h in range(4):
                if h < 3:
                    xa_sc(h + 1)
                xa_av(h)
            proj_post(p, wo, 8, lambda kc: oT[:, kc, :], pd, ff, sq, ones, pss, rbc, gpost, hb)
            p.dma("sp", HTv[:, :, c0:c0 + 512], hb[:])


C_TRI, C_BLK, C_SUP, C_SEL, C_IDB, C_SLO, C_MKQ, C_INVF, C_LINC, NCST = 0, 128, 256, 384, 896, 1024, 1152, 1280, 1281, 1409


def make_cst():
    j = np.arange(128)[:, None]
    l = np.arange(128)[None, :]
    same = (j // 64) == (l // 64)
    c = np.zeros((128, NCST), np.float32)
    c[:, C_TRI:C_TRI + 128] = (same & (j <= l))
    c[:, C_BLK:C_BLK + 128] = same
    c[:, C_SUP:C_SUP + 128] = (same & (j > l))
    for ch in range(2):
        for g in range(2):
            blk = np.zeros((128, 128), np.float32)
            blk[ch * 64 + 63, g * 64:(g + 1) * 64] = 1.0
            c[:, C_SEL + (ch * 2 + g) * 128:C_SEL + (ch * 2 + g + 1) * 128] = blk
    c[:, C_IDB:C_IDB + 128] = np.eye(128)
    c[:, C_SLO:C_SLO + 128] = (same & (j < l))
    c[:, C_MKQ:C_MKQ + 128] = ((j // 64) <= (l // 64))
    c[:, C_LINC:C_LINC + 128] = (same & (l <= j))
    inv = 1.0 / (10000.0 ** (np.arange(0, 32, 2, dtype=np.float32) / 32.0))
    c[0:32, C_INVF] = np.concatenate([inv, inv]) / (2.0 * np.pi)
    return c


def softplus_small(p, out, x, tmp):
    p.stt(tmp, x, -1.0, x, ALU.mult, ALU.max)
    p.act(tmp, tmp, AF.Exp, scale=-1.0)
    p.act(tmp, tmp, AF.Ln, bias=1.0)
    p.stt(out, x, 0.0, tmp, ALU.max, ALU.add)


def ssd_phase(ctx, W, L, COLF, COLT, COLS, YT):
    S = ctx.S
    NT = S // 128
    with ctx.phase() as p:
        pp = p.sb("pp", [128, NPP])
        p.dma("sp", pp[:], W["pp"][L])
        ident = p.sb("ident", [128, 128])
        p.dma("sp", ident[:], W["ident"])
        cst = p.sb("cst", [128, NCST])
        p.dma("sp", cst[:], W["cst"])
        TRI = cst[:, C_TRI:C_TRI + 128]
        BLK = cst[:, C_BLK:C_BLK + 128]
        SUP = cst[:, C_SUP:C_SUP + 128]
        identb = p.sb("identb", [128, 128], BF16)
        p.copy("dve", identb[:], cst[:, C_IDB:C_IDB + 128])
        tri3 = p.sb("tri3", [128, 1, 128])
        p.copy("dve", tri3[:, 0, :], TRI)
        a_bc = p.sb("a_bc", [128, 1, 8])
        p.act(a_bc[:, 0, :], pp[:, PP_SALOG:PP_SALOG + 8], AF.Exp)
        p.ts("dve", a_bc[:, 0, :], a_bc[:, 0, :], -1.0, ALU.mult)
        dsk = p.sb("dsk", [128, 8, 1])
        p.copy("dve", dsk[:, :, 0], pp[:, PP_SD:PP_SD + 8])
        dtb = p.sb("dtb", [128, 1, 8])
        p.copy("dve", dtb[:, 0, :], pp[:, PP_SDTB:PP_SDTB + 8])
        dtr = p.sb("dtr", [128, NT, CS_COLS])
        p.dma("sp", dtr[:], COLS.rearrange("(n p) c -> p n c", p=128))
        xall = p.sb("xall", [128, NT, 8])
        tmpa = p.sb("tmpa", [128, NT, 8])
        dt_all = p.sb("dt_all", [128, NT, 8])
        dA_all = p.sb("dA_all", [128, NT, 8])
        p.tt("dve", xall[:], dtr[:, :, CS_DT:CS_DT + 8], dtb[:].broadcast_to([128, NT, 8]), ALU.add)
        softplus_small(p, dt_all[:], xall[:], tmpa[:])
        p.tt("dve", dA_all[:], dt_all[:], a_bc[:].broadcast_to([128, NT, 8]), ALU.mult)
        Sf = p.sb("Sf", [128, 4, 64])
        Sz = [[p.sb("Sz%d_%d" % (i, gg), [128, 256], BF16) for gg in range(2)] for i in range(2)]
        p.memset("dve", Sf[:], 0.0)
        for i in range(2):
            for gg in range(2):
                p.memset("dve", Sz[i][gg][:], 0.0)
        cm_z = [p.sb("cmz%d" % gg, [128, 512], BF16) for gg in range(2)]
        xdte_z = [p.sb("xdtez%d" % ch, [128, 8, 64], BF16) for ch in range(2)]
        for gg in range(2):
            p.memset("dve", cm_z[gg][:], 0.0)
            p.memset("dve", xdte_z[gg][:], 0.0)
        xw = [p.sb("xw%d" % i, [128, 6, 515]) for i in range(2)]
        acc = p.sb("acc", [128, 512])
        xbcs = p.sb("xbcs", [128, 6, 512])
        bc_bf = p.sb("bc_bf", [128, 2, 512], BF16)
        ct = [p.sb("ct%d" % i, [128, 512]) for i in range(2)]
        xs_t = p.sb("xs_t", [128, 8, 64])
        bm_t = p.sb("bm_t", [128, 128], BF16)
        ac = p.sb("ac", [128, 24])
        E3 = p.sb("E3", [128, 24, 1])
        dte2 = p.sb("dte2", [128, 8, 1])
        R = p.sb("R", [128, 8, 128])
        Eseg = p.sb("Eseg", [128, 8, 128])
        cbm = p.sb("cbm", [128, 2, 128])
        G = p.sb("G", [128, 8, 128], BF16)
        xdt = p.sb("xdt", [128, 8, 64], BF16)
        dec = p.sb("dec", [128, 2, 4, 1])
        t1 = p.sb("t1", [128, 8, 64])
        t2 = p.sb("t2", [128, 8, 64])
        zs = p.sb("zs", [128, 512])
        junk = p.sb("junk", [128, 256])
        st = p.sb("st", [128, 4])
        yn = p.sb("yn", [128, 512], BF16)
        yg = [p.sb("yg%d" % i, [128, 4, 512], BF16) for i in range(2)]
        pX = p.ps("pX")
        pM = p.ps("pM")
        pS = [p.ps("pS%d" % i) for i in range(2)]
        pY = p.ps("pY")
        pO = p.ps("pO")
        pCS = p.ps("pCS")
        pXb = pX[:].bitcast(BF16)
        pconv = [p.ps("pconv"), pCS]
        dg = p.sb("dg", [128, 24, 128])
        for i in range(24):
            p.ts("dve", dg[:, i, :], ident[:], pp[:, PP_SCW + i:PP_SCW + i + 1], ALU.mult)
        dA3 = p.sb("dA3", [128, 8, 1])
        dt3 = p.sb("dt3", [128, 8, 1])
        for g in range(S // 512):
            c0 = g * 512
            w_ = xw[g % 2]
            p.dma("sp", w_[:, :, 3:515], COLF[CF_XBC:CF_XBC + 768, c0:c0 + 512].rearrange("(c p) t -> p c t", p=128))
            if g == 0:
                p.memset("dve", w_[:, :, 0:3], 0.0)
            else:
                with p.nc.allow_non_contiguous_dma(reason="3-token conv halo"):
                    p.dma("sp", w_[:, :, 0:3], COLF[CF_XBC:CF_XBC + 768, c0 - 3:c0].rearrange("(c p) t -> p c t", p=128))
            for c in range(6):
                pcv = pconv[c % 2]
                for k in range(4):
                    p.mm(pcv[:], dg[:, c * 4 + k, :], w_[:, c, k:k + 512], start=(k == 0), stop=(k == 3))
                p.act(xbcs[:, c, :], pcv[:], AF.Silu, bias=pp[:, PP_SCB + c:PP_SCB + c + 1])
            p.copy("dve", bc_bf[:], xbcs[:, 4:6, :])
            for gg in range(2):
                p.copy("dve", cm_z[gg][gg * 64:(gg + 1) * 64, :], xbcs[gg * 64:(gg + 1) * 64, 5, :])
            ygb = yg[g % 2]
            for tt in range(4):
                ti = g * 4 + tt
                tl = slice(tt * 128, (tt + 1) * 128)
                ctb = ct[tt % 2]
                p.dma("sp", ctb[:], COLT[c0 + tt * 128:c0 + (tt + 1) * 128, CT_ZS:CT_ZS + 512])
                if DEBUG.get('SSD', 99) < 2:
                    continue
                for c in range(4):
                    p.tr(pX[:, c * 128:(c + 1) * 128], xbcs[:, c, tl], ident[:])
                p.copy("act", xs_t[:].rearrange("p h d -> p (h d)"), pX[:])
                p.tr(pM[:, 0:128], xbcs[:, 4, tl], ident[:])
                p.copy("dve", bm_t[:], pM[:, 0:128])
                if DEBUG.get('SSD', 99) < 3:
                    continue
                p.copy("dve", dA3[:, :, 0], dA_all[:, ti, :])
                p.mm(pM[:, 128:136], TRI, dA_all[:, ti, :])
                p.mm(pM[:, 136:144], BLK, dA_all[:, ti, :])
                p.copy("dve", ac[:, 0:8], pM[:, 128:136])
                p.copy("dve", ac[:, 16:24], pM[:, 136:144])
                p.tt("dve", ac[:, 8:16], ac[:, 16:24], ac[:, 0:8], ALU.subtract)
                p.act(E3[:, :, 0], ac[:], AF.Exp)
                p.tt("dve", dte2[:, :, 0], dt_all[:, ti, :], E3[:, 8:16, 0], ALU.mult)
                if DEBUG.get('SSD', 99) < 4:
                    continue
                p.tt("dve", R[:], tri3[:].broadcast_to([128, 8, 128]), dA3[:].broadcast_to([128, 8, 128]), ALU.mult)
                for hh in range(2):
                    p.mm(pS[hh][:], SUP, R[:, hh * 4:(hh + 1) * 4, :].rearrange("p h l -> p (h l)"))
                    p.act(Eseg[:, hh * 4:(hh + 1) * 4, :].rearrange("p h l -> p (h l)"), pS[hh][:], AF.Exp)
                if DEBUG.get('SSD', 99) < 5:
                    continue
                for gg in range(2):
                    p.mm(pM[:, 256 + gg * 128:256 + (gg + 1) * 128], bc_bf[:, 0, tl], cm_z[gg][:, tl])
                p.tt("dve", cbm[:], pM[:, 256:512].rearrange("p (g l) -> p g l", g=2), tri3[:].broadcast_to([128, 2, 128]), ALU.mult)
                for gg in range(2):
                    p.tt("dve", G[:, gg * 4:(gg + 1) * 4, :], Eseg[:, gg * 4:(gg + 1) * 4, :],
                         cbm[:, gg:gg + 1, :].broadcast_to([128, 4, 128]), ALU.mult)
                if DEBUG.get('SSD', 99) < 6:
                    continue
                p.copy("dve", dt3[:, :, 0], dt_all[:, ti, :])
                p.tt("dve", xdt[:], xs_t[:], dt3[:].broadcast_to([128, 8, 64]), ALU.mult)
                for ch in range(2):
                    hs = slice(ch * 64, (ch + 1) * 64)
                    p.tt("dve", xdte_z[ch][hs], xs_t[hs], dte2[hs].broadcast_to([64, 8, 64]), ALU.mult)
                for h in range(8):
                    p.mm(pY[:, h * 64:(h + 1) * 64], G[:, h, :], xdt[:, h, :])
                if DEBUG.get('SSD', 99) < 7:
                    continue
                for ch in range(2):
                    for gg in range(2):
                        p.mm(pCS[gg * 64:(gg + 1) * 64, ch * 256:(ch + 1) * 256],
                             bm_t[:, gg * 64:(gg + 1) * 64],
                             xdte_z[ch][:, gg * 4:(gg + 1) * 4, :].rearrange("p h d -> p (h d)"))
                for ch in range(2):
                    for gg in range(2):
                        sel = cst[:, C_SEL + (ch * 2 + gg) * 128:C_SEL + (ch * 2 + gg + 1) * 128]
                        p.mm(pM[:, 144 + ch * 4:144 + ch * 4 + 4], sel, E3[:, 16 + gg * 4:16 + gg * 4 + 4, 0],
                             start=(gg == 0), stop=(gg == 1))
                p.copy("dve", dec[:].rearrange("p c h a -> p (c h a)"), pM[:, 144:152])
                if DEBUG.get('SSD', 99) < 8:
                    continue
                for ch in range(2):
                    for gg in range(2):
                        p.mm(pO[ch * 64:(ch + 1) * 64, gg * 256:(gg + 1) * 256],
                             bc_bf[:, 1, tt * 128 + ch * 64:tt * 128 + (ch + 1) * 64], Sz[ch % 2][gg][:])
                    p.tt("dve", Sf[:], Sf[:], dec[:, ch, :, :].broadcast_to([128, 4, 64]), ALU.mult)
                    p.tt("dve", Sf[:], Sf[:], pCS[:, ch * 256:(ch + 1) * 256].rearrange("p (h d) -> p h d", h=4), ALU.add)
                    for gg in range(2):
                        p.copy("act" if gg == 0 else "dve", Sz[(ch + 1) % 2][gg][gg * 64:(gg + 1) * 64, :],
                               Sf[gg * 64:(gg + 1) * 64].rearrange("p h d -> p (h d)"))
                if DEBUG.get('SSD', 99) < 9:
                    continue
                p.tt("dve", t1[:], pO[:].rearrange("p (h d) -> p h d", h=8), E3[:, 0:8, :].broadcast_to([128, 8, 64]), ALU.mult)
                p.tt("dve", t1[:], t1[:], pY[:].rearrange("p (h d) -> p h d", h=8), ALU.add)
                p.tt("pool", t2[:], xs_t[:], dsk[:].broadcast_to([128, 8, 64]), ALU.mult)
                p.tt("dve", t1[:], t1[:], t2[:], ALU.add)
                if DEBUG.get('SSD', 99) < 10:
                    continue
                p.act(zs[:], ctb[:], AF.Silu)
                t1f = t1[:].rearrange("p h d -> p (h d)")
                p.tt("dve", t1f, t1f, zs[:], ALU.mult)
                for gg in range(2):
                    p.act(junk[:], t1f[:, gg * 256:(gg + 1) * 256], AF.Square, accum=st[:, gg:gg + 1])
                p.rstd(st[:, 2:4], st[:, 0:2], 1.0 / 256)
                for gg in range(2):
                    p.stt(yn[:, gg * 256:(gg + 1) * 256], t1f[:, gg * 256:(gg + 1) * 256], st[:, 2 + gg:3 + gg],
                          pp[:, PP_SNG + gg * 256:PP_SNG + (gg + 1) * 256], ALU.mult, ALU.mult)
                if DEBUG.get('SSD', 99) < 11:
                    continue
                for c in range(4):
                    p.tr(pXb[:, c * 128:(c + 1) * 128], yn[:, c * 128:(c + 1) * 128], identb[:])
                p.copy("act", ygb[:, :, tl], pXb[:, 0:512].rearrange("p (c t) -> p c t", c=4))
            p.dma("sp", YT[0:512, c0:c0 + 512].rearrange("(c p) t -> p c t", p=128), ygb[:])


def rope_phase(ctx, W, pos, ROPE):
    S = ctx.S
    TWO_PI = 2.0 * np.pi * (1.0 - 2e-6)
    with ctx.phase() as p:
        cst = p.sb("cst", [128, NCST])
        p.dma("sp", cst[:], W["cst"])
        posi = p.sb("posi", [32, S], I32)
        p.dma("sp", posi[:], pos.partition_broadcast(32))
        t = p.sb("t", [32, S])
        ti = p.sb("ti", [32, S], I32)
        tf = p.sb("tf", [32, S])
        f = p.sb("f", [32, S])
        m = p.sb("m", [32, S])
        o = p.sb("o", [32, S])
        pib = p.sb("pib", [32, 1])
        p.memset("dve", pib[:], float(np.pi * (1.0 - 2e-6)))
        p.copy("dve", t[:], posi[:])
        p.ts("dve", t[:], t[:], cst[0:32, C_INVF:C_INVF + 1], ALU.mult)
        for which in (1, 0):
            if which == 0:
                p.ts("dve", t[:], t[:], 0.25, ALU.add)
            p.copy("dve", ti[:], t[:])
            p.copy("dve", tf[:], ti[:])
            p.tt("dve", f[:], t[:], tf[:], ALU.subtract)
            p.ts("dve", m[:], f[:], 0.0, ALU.is_lt)
            p.tt("dve", f[:], f[:], m[:], ALU.add)
            p.ts("dve", m[:], f[:], 1.0, ALU.is_ge)
            p.tt("dve", f[:], f[:], m[:], ALU.subtract)
            p.act(o[:], f[:], AF.Sin, scale=-TWO_PI, bias=pib[:])
            p.dma("sp", ROPE[which], o[:])


def mla_phase(ctx, W, L, COLF, ROPE, YT):
    S = ctx.S
    NT = S // 128
    NG = S // 512
    SCALE = 96.0 ** -0.5
    with ctx.phase() as p:
        pp = p.sb("pp", [128, NPP])
        p.dma("sp", pp[:], W["pp"][L])
        cst = p.sb("cst", [128, NCST])
        p.dma("sp", cst[:], W["cst"])
        ones = p.sb("ones", [128, 128], BF16)
        p.memset("dve", ones[:], 1.0)
        mkq = p.sb("mkq", [128, 128], BF16)
        p.copy("dve", mkq[:], cst[:, C_MKQ:C_MKQ + 128])
        wuq = p.sb("wuq", [128, 2, 384], BF16)
        p.dma("pool", wuq[:], W["mla_w_uq"][L].rearrange("(c p) f -> p c f", p=128))
        wukv = p.sb("wukv", [128, 512], BF16)
        p.dma("pool", wukv[:], W["mla_w_ukv"][L])
        wrot = p.sb("wrot", [128, 2, 4, 32], BF16)
        wv = p.sb("wv", [128, 4, 64], BF16)
        for h in range(4):
            b = h * 96 + 64
            p.ts("dve", wrot[:, :, h, 0:16], wuq[:, :, b + 16:b + 32], -1.0, ALU.mult)
            p.copy("dve", wrot[:, :, h, 16:32], wuq[:, :, b:b + 16])
            p.copy("dve", wv[:, h, :], wukv[:, h * 128 + 64:h * 128 + 128])
        kT = [p.sb("kT%d" % h, [128, S], BF16) for h in range(4)]
        qT = [p.sb("qT%d" % h, [128, 512], BF16) for h in range(4)]
        for h in range(4):
            p.memset("dve", kT[h][:], 0.0)
            p.memset("dve", qT[h][:], 0.0)
        vt = p.sb("vt", [128, NT, 4, 128], BF16)
        p.memset("dve", vt[:], 0.0)
        ckvn = p.sb("ckvn", [128, 512], BF16)
        cin = p.sb("cin", [128, 2, 512])
        sq = p.sb("sq", [128, 2, 512], BF16)
        cqn = p.sb("cqn", [128, 2, 512], BF16)
        rbc = p.sb("rbc", [128, 512])
        cs2 = p.sb("cs2", [128, 2, 512])
        kr = p.sb("kr", [128, 2, 512])
        r1 = p.sb("r1", [128, 512])
        r2 = p.sb("r2", [128, 512])
        krf = p.sb("krf", [128, 512], BF16)
        PT = [p.sb("PT%d" % i, [128, 512], BF16) for i in range(3)]
        rb = p.sb("rb", [128, 512])
        oT = [p.sb("oT%d" % i, [128, 512], BF16) for i in range(2)]
        pss = p.ps("pss")
        pq = [p.ps("pq%d" % i) for i in range(3)]
        ps = [p.ps("ps%d" % i) for i in range(2)]
        pA = p.ps("pA")
        pB = p.ps("pB")
        R32 = slice(64, 96)
        ps3 = [ps[0], ps[1], pq[2]]

        def load_cs(c0):
            p.dma("sp", cs2[R32, 0, :], ROPE[0, :, c0:c0 + 512])
            p.dma("sp", cs2[R32, 1, :], ROPE[1, :, c0:c0 + 512])

        for g in range(NG):
            c0 = g * 512
            cols = slice(c0, c0 + 512)
            load_cs(c0)
            p.dma("sp", cin[:, 0, :], COLF[CF_CKV:CF_CKV + 128, cols])
            p.act(sq[:, 0, :], cin[:, 0, :], AF.Square)
            p.mm(pss[:], ones[:], sq[:, 0, :])
            p.rstd(rbc[:], pss[:], 1.0 / 128)
            p.stt(ckvn[:], cin[:, 0, :], pp[:, PP_KVG:PP_KVG + 1], rbc[:], ALU.mult, ALU.mult)
            for h in range(4):
                p.mm(pq[h % 2][0:64, :], wukv[:, h * 128:h * 128 + 64], ckvn[:])
                p.copy("act" if h % 2 == 0 else "dve", kT[h][0:64, cols], pq[h % 2][0:64, :])
            for tt in range(4):
                p.mm(pq[2][:, 0:256], ckvn[:, tt * 128:(tt + 1) * 128], wv[:].rearrange("p h d -> p (h d)"))
                p.copy("act" if tt % 2 == 0 else "dve", vt[:, g * 4 + tt, :, 0:64], pq[2][:, 0:256].rearrange("p (h d) -> p h d", h=4))
            p.dma("sp", kr[R32, 0, :], COLF[CF_KR:CF_KR + 32, cols])
            p.dma("sp", kr[64:80, 1, :], COLF[CF_KR + 16:CF_KR + 32, cols])
            p.dma("sp", kr[80:96, 1, :], COLF[CF_KR:CF_KR + 16, cols])
            p.ts("dve", kr[64:80, 1, :], kr[64:80, 1, :], -1.0, ALU.mult)
            p.tt("dve", r1[R32], kr[R32, 0, :], cs2[R32, 0, :], ALU.mult)
            p.tt("dve", r2[R32], kr[R32, 1, :], cs2[R32, 1, :], ALU.mult)
            p.tt("dve", krf[R32], r1[R32], r2[R32], ALU.add)
            for h in range(4):
                p.copy("act" if h % 2 == 0 else "dve", kT[h][R32, cols], krf[R32])
        for G in range(NG):
            c0 = G * 512
            load_cs(c0)
            p.dma("sp", cin[:], COLF[CF_CQ:CF_CQ + 256, c0:c0 + 512].rearrange("(c p) t -> p c t", p=128))
            p.act(sq[:], cin[:], AF.Square)
            for kc in range(2):
                p.mm(pss[:], ones[:], sq[:, kc, :], start=(kc == 0), stop=(kc == 1))
            p.rstd(rbc[:], pss[:], 1.0 / 256)
            for kc in range(2):
                p.stt(cqn[:, kc, :], cin[:, kc, :], pp[:, PP_QG + kc:PP_QG + kc + 1], rbc[:], ALU.mult, ALU.mult)
            for h in range(4):
                b = h * 96
                for kc in range(2):
                    p.mm(pq[0][0:64, :], wuq[:, kc, b:b + 64], cqn[:, kc, :], start=(kc == 0), stop=(kc == 1))
                for kc in range(2):
                    p.mm(pq[1][R32, :], wuq[:, kc, b + 64:b + 96], cqn[:, kc, :], start=(kc == 0), stop=(kc == 1))
                for kc in range(2):
                    p.mm(pq[2][R32, :], wrot[:, kc, h, :], cqn[:, kc, :], start=(kc == 0), stop=(kc == 1))
                p.act(qT[h][0:64, :], pq[0][0:64, :], AF.Copy, scale=SCALE)
                p.tt("dve", r1[R32], pq[1][R32, :], cs2[R32, 0, :], ALU.mult)
                p.tt("dve", r2[R32], pq[2][R32, :], cs2[R32, 1, :], ALU.mult)
                p.tt("dve", r1[R32], r1[R32], r2[R32], ALU.add)
                p.ts("dve", qT[h][R32, :], r1[R32], SCALE, ALU.mult)
            for h in range(4):
                nkt = 4 * G + 4
                accA, accB = (pA, pB) if h % 2 == 0 else (pq[0], pq[1])

                def score(kt, h=h, G=G):
                    r = kt - 4 * G
                    q0 = max(r, 0) * 128
                    psb, ptb = ps3[kt % 3], PT[kt % 3]
                    p.mm(psb[:, q0:512], kT[h][:, kt * 128:(kt + 1) * 128], qT[h][:, q0:512])
                    p.act(ptb[:, q0:512], psb[:, q0:512], AF.Exp)
                    if r >= 0:
                        p.tt("dve", ptb[:, q0:q0 + 128], ptb[:, q0:q0 + 128], mkq[:], ALU.mult)

                def pv(kt, h=h, G=G, nkt=nkt, accA=accA, accB=accB):
                    r = kt - 4 * G
                    q0 = max(r, 0) * 128
                    ptb = PT[kt % 3]
                    p.mm(accA[:, q0:512], vt[:, kt, h, :], ptb[:, q0:512], start=(kt == 0), stop=(kt == nkt - 1))
                    p.mm(accB[:, q0:512], ones[:], ptb[:, q0:512], start=(kt == 0), stop=(kt == nkt - 1))

                score(0)
                if nkt > 1:
                    score(1)
                for kt in range(nkt):
                    if kt + 2 < nkt:
                        score(kt + 2)
                    pv(kt)
                p.recip(rb[0:64, :], accB[0:64, :])
                ob = oT[h % 2]
                p.tt("dve", ob[0:64, :], accA[0:64, :], rb[0:64, :], ALU.mult)
                p.dma("sp", YT[512 + h * 64:512 + (h + 1) * 64, c0:c0 + 512], ob[0:64, :])


def gdn_phase(ctx, W, L, COLF, COLT, COLS, YT):
    S = ctx.S
    NT = S // 128
    with ctx.phase() as p:
        pp = p.sb("pp", [128, NPP])
        p.dma("sp", pp[:], W["pp"][L])
        ident = p.sb("ident", [128, 128])
        p.dma("sp", ident[:], W["ident"])
        cst = p.sb("cst", [128, NCST])
        p.dma("sp", cst[:], W["cst"])
        TRI = cst[:, C_TRI:C_TRI + 128]
        BLK = cst[:, C_BLK:C_BLK + 128]
        identb = p.sb("identb", [128, 1, 128], BF16)
        p.copy("dve", identb[:, 0, :], cst[:, C_IDB:C_IDB + 128])
        blkb = p.sb("blkb", [128, 128], BF16)
        p.copy("dve", blkb[:], BLK)
        sup3 = p.sb("sup3", [128, 1, 128])
        p.copy("dve", sup3[:, 0, :], cst[:, C_SUP:C_SUP + 128])
        linc3 = p.sb("linc3", [128, 1, 128])
        p.copy("dve", linc3[:, 0, :], cst[:, C_LINC:C_LINC + 128])
        ngb = p.sb("ngb", [128, 1, 64])
        p.copy("dve", ngb[:, 0, :], pp[:, PP_GNG:PP_GNG + 64])
        dtr = p.sb("dtr", [128, NT, CS_COLS])
        p.dma("sp", dtr[:], COLS.rearrange("(n p) c -> p n c", p=128))
        na_bc = p.sb("na_bc", [128, 1, 4])
        p.act(na_bc[:, 0, :], pp[:, PP_GALOG:PP_GALOG + 4], AF.Exp)
        p.ts("dve", na_bc[:, 0, :], na_bc[:, 0, :], -1.0, ALU.mult)
        dtb = p.sb("dtb", [128, 1, 4])
        p.copy("dve", dtb[:, 0, :], pp[:, PP_GDTB:PP_GDTB + 4])
        xall = p.sb("xall", [128, NT, 4])
        tmpa = p.sb("tmpa", [128, NT, 4])
        g_all = p.sb("g_all", [128, NT, 4])
        beta_all = p.sb("beta_all", [128, NT, 4])
        p.tt("dve", xall[:], dtr[:, :, CS_A:CS_A + 4], dtb[:].broadcast_to([128, NT, 4]), ALU.add)
        softplus_small(p, g_all[:], xall[:], tmpa[:])
        p.tt("dve", g_all[:], g_all[:], na_bc[:].broadcast_to([128, NT, 4]), ALU.mult)
        p.act(beta_all[:], dtr[:, :, CS_B:CS_B + 4], AF.Sigmoid)
        Sf = [p.sb("Sf%d" % c, [128, 64]) for c in range(2)]
        Sb = [p.sb("Sb%d" % c, [128, 64], BF16) for c in range(2)]
        for c in range(2):
            p.memset("dve", Sf[c][:], 0.0)
            p.memset("dve", Sb[c][:], 0.0)
        xw = [p.sb("xw%d" % i, [128, 6, 515]) for i in range(2)]
        acc = p.sb("acc", [128, 512])
        qkv = p.sb("qkv", [128, 6, 512])
        sq = p.sb("sq", [128, 512], BF16)
        rin = p.sb("rin", [128, 512])
        nT = p.sb("nT", [128, 4, 512], BF16)
        nTz = [p.sb("nTz%d" % i, [128, 512], BF16) for i in range(8)]
        for i in range(8):
            p.memset("dve", nTz[i][:], 0.0)
        zt = [p.sb("zt%d" % i, [128, 256]) for i in range(2)]
        kn_t = p.sb("kn_t", [128, 4, 64])
        v_t = p.sb("v_t", [128, 4, 64])
        g3 = p.sb("g3", [128, 4, 1])
        gc = p.sb("gc", [128, 12])
        E3 = p.sb("E3", [128, 12, 1])
        be = p.sb("be", [128, 4, 1])
        nb = p.sb("nb", [128, 4, 1])
        b3 = p.sb("b3", [128, 4, 1])
        R2 = p.sb("R2", [128, 4, 128])
        E = p.sb("E", [128, 4, 128])
        Gs = p.sb("Gs", [128, 4, 128])
        tmpK = p.sb("tmpK", [128, 4, 128])
        QKm = p.sb("QKm", [128, 4, 128], BF16)
        QKmT = p.sb("QKmT", [128, 4, 128], BF16)
        A = [p.sb("A%d" % i, [128, 4, 128]) for i in range(2)]
        B = [p.sb("B%d" % i, [128, 4, 128]) for i in range(2)]
        P = [p.sb("P%d" % i, [128, 4, 128]) for i in range(2)]
        bv = p.sb("bv", [128, 4, 64])
        bke = p.sb("bke", [128, 4, 64])
        kdz = [p.sb("kdz%d" % i, [128, 4, 64], BF16) for i in range(2)]
        wTz = p.sb("wTz", [128, 4, 128], BF16)
        vnew = p.sb("vnew", [128, 4, 64], BF16)
        for t_ in (kdz[0], kdz[1], wTz, vnew):
            p.memset("dve", t_[:], 0.0)
        u = p.sb("u", [128, 4, 64])
        o1 = p.sb("o1", [128, 4, 64])
        o = p.sb("o", [128, 4, 64])
        osq = p.sb("osq", [128, 4, 64])
        st = p.sb("st", [128, 8])
        zs = p.sb("zs", [128, 256])
        yn = p.sb("yn", [128, 256], BF16)
        dec = p.sb("dec", [128, 4])
        yg = [p.sb("yg%d" % i, [128, 2, 512], BF16) for i in range(2)]
        bk = [p.ps("b%d" % i) for i in range(8)]
        b0b = bk[0][:].bitcast(BF16)
        b1b = bk[1][:].bitcast(BF16)
        dg = p.sb("dg", [128, 24, 128])
        for i in range(24):
            p.ts("dve", dg[:, i, :], ident[:], pp[:, PP_GCW + i:PP_GCW + i + 1], ALU.mult)
        ident3 = p.sb("ident3", [128, 1, 128])
        p.copy("dve", ident3[:, 0, :], ident[:])
        for g in range(S // 512):
            c0 = g * 512
            w_ = xw[g % 2]
            p.dma("sp", w_[:, :, 3:515], COLF[CF_QKV:CF_QKV + 768, c0:c0 + 512].rearrange("(c p) t -> p c t", p=128))
            if g == 0:
                p.memset("dve", w_[:, :, 0:3], 0.0)
            else:
                with p.nc.allow_non_contiguous_dma(reason="3-token conv halo"):
                    p.dma("sp", w_[:, :, 0:3], COLF[CF_QKV:CF_QKV + 768, c0 - 3:c0].rearrange("(c p) t -> p c t", p=128))
            for c in range(6):
                pcv = bk[5 + c % 2]
                for k in range(4):
                    p.mm(pcv[:], dg[:, c * 4 + k, :], w_[:, c, k:k + 512], start=(k == 0), stop=(k == 3))
                p.act(qkv[:, c, :], pcv[:], AF.Silu)
            for c in range(4):
                p.act(sq[:], qkv[:, c, :], AF.Square)
                p.mm(bk[1][:], blkb[:], sq[:])
                p.rstd(rin[:], bk[1][:], 1.0)
                if c < 2:
                    p.stt(nT[:, c, :], qkv[:, c, :], 0.125, rin[:], ALU.mult, ALU.mult)
                else:
                    p.tt("dve", nT[:, c, :], qkv[:, c, :], rin[:], ALU.mult)
                for par in range(2):
                    hs = slice(par * 64, (par + 1) * 64)
                    hidx = (c % 2) * 2 + par + (0 if c < 2 else 4)
                    p.copy("act" if par == 0 else "dve", nTz[hidx][hs, :], nT[hs, c, :])
            ygb = yg[g % 2]
            for tt in range(4):
                ti = g * 4 + tt
                tl = slice(tt * 128, (tt + 1) * 128)
                ztb = zt[tt % 2]
                p.dma("sp", ztb[:], COLT[c0 + tt * 128:c0 + (tt + 1) * 128, CT_ZG:CT_ZG + 256])
                for c in range(2):
                    p.tr(b0b[:, c * 128:(c + 1) * 128], nT[:, 2 + c, tl], identb[:, 0, :])
                p.copy("act", kn_t[:].rearrange("p h d -> p (h d)"), b0b[:, 0:256])
                for c in range(2):
                    p.tr(bk[0][:, c * 128:(c + 1) * 128], qkv[:, 4 + c, tl], ident[:])
                p.copy("dve", v_t[:].rearrange("p h d -> p (h d)"), bk[0][:, 0:256])
                p.copy("dve", g3[:, :, 0], g_all[:, ti, :])
                p.copy("dve", b3[:, :, 0], beta_all[:, ti, :])
                p.mm(bk[1][:, 0:4], TRI, g_all[:, ti, :])
                p.mm(bk[1][:, 4:8], BLK, g_all[:, ti, :])
                p.copy("dve", gc[:, 0:4], bk[1][:, 0:4])
                p.copy("dve", gc[:, 8:12], bk[1][:, 4:8])
                p.tt("dve", gc[:, 4:8], gc[:, 8:12], gc[:, 0:4], ALU.subtract)
                p.act(E3[:, :, 0], gc[:], AF.Exp)
                p.tt("dve", be[:], b3[:], E3[:, 0:4, :], ALU.mult)
                p.ts("dve", nb[:], b3[:], -1.0, ALU.mult)
                p.tt("dve", R2[:], sup3[:].broadcast_to([128, 4, 128]), g3[:].broadcast_to([128, 4, 128]), ALU.mult)
                p.mm(bk[2][:], TRI, R2[:].rearrange("p h s -> p (h s)"))
                p.act(E[:].rearrange("p h s -> p (h s)"), bk[2][:], AF.Exp)
                for h in range(4):
                    p.mm(bk[3][:, h * 128:(h + 1) * 128], nTz[4 + h][:, tl], nTz[4 + h][:, tl])
                for h in range(4):
                    p.mm(bk[4][:, h * 128:(h + 1) * 128], nTz[h][:, tl], nTz[4 + h][:, tl])
                p.tt("dve", Gs[:], E[:], sup3[:].broadcast_to([128, 4, 128]), ALU.mult)
                p.tt("dve", tmpK[:], bk[3][:].rearrange("p (h s) -> p h s", h=4), nb[:].broadcast_to([128, 4, 128]), ALU.mult)
                p.tt("dve", A[0][:], tmpK[:], Gs[:], ALU.mult)
                p.tt("dve", Gs[:], E[:], linc3[:].broadcast_to([128, 4, 128]), ALU.mult)
                p.tt("dve", QKm[:], bk[4][:].rearrange("p (h s) -> p h s", h=4), Gs[:], ALU.mult)
                for h in range(4):
                    p.tr(bk[0][:, h * 128:(h + 1) * 128], A[0][:, h, :], ident[:])
                p.copy("act", B[0][:].rearrange("p h s -> p (h s)"), bk[0][:])
                p.tt("dve", P[0][:], B[0][:], ident3[:].broadcast_to([128, 4, 128]), ALU.add)
                for h in range(4):
                    p.tr(b1b[:, 512 + h * 128:512 + (h + 1) * 128], QKm[:, h, :], identb[:, 0, :])
                p.copy("act", QKmT[:].rearrange("p h s -> p (h s)"), b1b[:, 512:1024])
                for k in range(1, 6):
                    Ao, Bo = A[(k - 1) % 2], B[(k - 1) % 2]
                    An, Bn = A[k % 2], B[k % 2]
                    Po, Pn = P[(k - 1) % 2], P[k % 2]
                    for h in range(4):
                        p.mm(bk[5][:, h * 128:(h + 1) * 128], Bo[:, h, :], Ao[:, h, :])
                    if k < 5:
                        for h in range(4):
                            p.mm(bk[6][:, h * 128:(h + 1) * 128], Ao[:, h, :], Bo[:, h, :])
                    p.copy("act", An[:].rearrange("p h s -> p (h s)"), bk[5][:])
                    if k < 5:
                        p.copy("dve", Bn[:].rearrange("p h s -> p (h s)"), bk[6][:])
                    for h in range(4):
                        p.mm(bk[7][:, h * 128:(h + 1) * 128], An[:, h, :], Po[:, h, :])
                    p.tt("dve", Pn[:], bk[7][:].rearrange("p (h s) -> p h s", h=4), Po[:], ALU.add)
                PT_ = P[5 % 2]
                p.tt("dve", bv[:], v_t[:], b3[:].broadcast_to([128, 4, 64]), ALU.mult)
                p.tt("dve", bke[:], kn_t[:], be[:].broadcast_to([128, 4, 64]), ALU.mult)
                for ch in range(2):
                    hs = slice(ch * 64, (ch + 1) * 64)
                    p.tt("dve", kdz[ch][hs], kn_t[hs], E3[hs, 4:8, :].broadcast_to([64, 4, 64]), ALU.mult)
                for h in range(4):
                    p.mm(bk[2][:, h * 64:(h + 1) * 64], PT_[:, h, :], bv[:, h, :])
                p.copy("act", u[:].rearrange("p h d -> p (h d)"), bk[2][:, 0:256])
                for h in range(4):
                    ps_ = slice((h % 2) * 64, (h % 2 + 1) * 64)
                    p.mm(bk[3][ps_, h * 128:(h + 1) * 128], bke[:, h, :], PT_[:, h, :])
                for h in range(4):
                    ps_ = slice((h % 2) * 64, (h % 2 + 1) * 64)
                    p.copy("dve" if h % 2 == 0 else "act", wTz[ps_, h, :], bk[3][ps_, h * 128:(h + 1) * 128])
                for ch in range(2):
                    for par in range(2):
                        sel = cst[:, C_SEL + (ch * 2 + par) * 128:C_SEL + (ch * 2 + par + 1) * 128]
                        p.mm(bk[1][:, 8 + ch * 2:8 + ch * 2 + 2], sel, E3[:, 8 + par:12:2, 0],
                             start=(par == 0), stop=(par == 1))
                p.copy("dve", dec[:], bk[1][:, 8:12])
                for ch in range(2):
                    rs = slice(ch * 64, (ch + 1) * 64)
                    cl = slice(tt * 128 + ch * 64, tt * 128 + (ch + 1) * 64)
                    lc = slice(ch * 64, (ch + 1) * 64)
                    for h in range(4):
                        p.mm(bk[4][rs, h * 64:(h + 1) * 64], wTz[:, h, lc], Sb[h // 2][:])
                    for h in range(4):
                        p.mm(bk[5][rs, h * 64:(h + 1) * 64], nTz[h][:, cl], Sb[h // 2][:])
                    p.tt("dve", vnew[rs].rearrange("p h d -> p (h d)"), u[rs].rearrange("p h d -> p (h d)"),
                         bk[4][rs, 0:256], ALU.subtract)
                    p.tt("dve", o1[rs], bk[5][rs, 0:256].rearrange("p (h d) -> p h d", h=4),
                         E3[rs, 0:4, :].broadcast_to([64, 4, 64]), ALU.mult)
                    for h in range(4):
                        p.mm(bk[6][rs, h * 64:(h + 1) * 64], QKmT[:, h, lc], vnew[:, h, :])
                    for h in range(4):
                        ps_ = slice((h % 2) * 64, (h % 2 + 1) * 64)
                        p.mm(bk[7][ps_, (h // 2) * 64:(h // 2 + 1) * 64], kdz[ch][:, h, :], vnew[:, h, :])
                    p.tt("dve", o[rs].rearrange("p h d -> p (h d)"), o1[rs].rearrange("p h d -> p (h d)"),
                         bk[6][rs, 0:256], ALU.add)
                    for c in range(2):
                        p.stt(Sf[c][:], Sf[c][:], dec[:, ch * 2 + c:ch * 2 + c + 1], bk[7][:, c * 64:(c + 1) * 64],
                              ALU.mult, ALU.add)
                        p.copy("act", Sb[c][:], Sf[c][:])
                p.tt("dve", osq[:], o[:], o[:], ALU.mult)
                p.s.add("dve", lambda e: e.tensor_reduce(st[:, 0:4], osq[:], AX.X, ALU.add), reads=[osq[:]], writes=[st[:, 0:4]])
                p.rstd(st[:, 4:8], st[:, 0:4], 1.0 / 64)
                p.copy("dve", nb[:, :, 0], st[:, 4:8])
                p.tt("dve", o[:], o[:], nb[:].broadcast_to([128, 4, 64]), ALU.mult)
                p.tt("dve", o[:], o[:], ngb[:].broadcast_to([128, 4, 64]), ALU.mult)
                p.act(zs[:], ztb[:], AF.Silu)
                p.tt("dve", yn[:], o[:].rearrange("p h d -> p (h d)"), zs[:], ALU.mult)
                for c in range(2):
                    p.tr(b0b[:, c * 128:(c + 1) * 128], yn[:, c * 128:(c + 1) * 128], identb[:, 0, :])
                p.copy("act", ygb[:, :, tl], b0b[:, 0:256].rearrange("p (c t) -> p c t", c=2))
            p.dma("sp", YT[768:1024, c0:c0 + 512].rearrange("(c p) t -> p c t", p=128), ygb[:])


WEIGHT_NAMES = ["ffn_w_up", "ffn_w_down", "w_in", "mla_w_uq", "mla_w_ukv", "w_out", "xa_w_q", "xa_w_kv", "xa_w_o"]


def build_program(S, depth, shapes):
    nc = bass.Bass("TRN2", target_bir_lowering=False)
    x = nc.dram_tensor("x", [S, D], F32, kind="ExternalInput").ap()
    mem = nc.dram_tensor("mem", [N_MEM, D], F32, kind="ExternalInput").ap()
    pos = nc.dram_tensor("positions", [S], I32, kind="ExternalInput").ap()
    W = {}
    for nm in WEIGHT_NAMES:
        W[nm] = nc.dram_tensor(nm, list(shapes[nm]), F32, kind="ExternalInput").ap()
    W["pp"] = nc.dram_tensor("pp", [depth, 128, NPP], F32, kind="ExternalInput").ap()
    W["cst"] = nc.dram_tensor("cst", [128, NCST], F32, kind="ExternalInput").ap()
    W["ident"] = nc.dram_tensor("ident", [128, 128], F32, kind="ExternalInput").ap()
    y = nc.dram_tensor("y", [S, D], F32, kind="ExternalOutput").ap()
    HT = nc.dram_tensor("HT", [D, S], F32).ap()
    COLF = nc.dram_tensor("COLF", [CF_ROWS, S], F32).ap()
    COLT = nc.dram_tensor("COLT", [S, CT_COLS], F32).ap()
    COLS = nc.dram_tensor("COLS", [S, CS_COLS], F32).ap()
    YT = nc.dram_tensor("YT", [D, S], BF16).ap()
    ROPE = nc.dram_tensor("ROPE", [2, 32, S], F32).ap()
    ctx = Ctx(nc, S)
    with ctx.stack:
        to_feature_major(ctx, W, x, HT)
        rope_phase(ctx, W, pos, ROPE)
        for L in range(depth):
            ffn_phase(ctx, W, L, 0, HT, HT)
            inproj_phase(ctx, W, L, HT, COLF, COLT, COLS)
            ssd_phase(ctx, W, L, COLF, COLT, COLS, YT)
            mla_phase(ctx, W, L, COLF, ROPE, YT)
            gdn_phase(ctx, W, L, COLF, COLT, COLS, YT)
            outproj_phase(ctx, W, L, HT, YT)
            xattn_phase(ctx, W, L, HT, mem)
            ffn_phase(ctx, W, L, 1, HT, HT)
        to_token_major(ctx, W, HT, y)
    return nc


def kernel(**inputs):
    inp = {k: np.asarray(v) for k, v in inputs.items()}
    B, S, _ = inp["x"].shape
    depth = inp["norm_g"].shape[0]
    shapes = {nm: inp[nm].shape for nm in WEIGHT_NAMES}
    nc = build_program(S, depth, shapes)
    shared = {nm: np.ascontiguousarray(inp[nm], dtype=np.float32) for nm in WEIGHT_NAMES}
    shared["pp"] = make_pp(inp, depth)
    shared["cst"] = make_cst()
    shared["ident"] = np.eye(128, dtype=np.float32)
    in_maps = []
    for b in range(B):
        m = dict(shared)
        m["x"] = np.ascontiguousarray(inp["x"][b], dtype=np.float32)
        m["mem"] = np.ascontiguousarray(inp["mem"][b], dtype=np.float32)
        m["positions"] = np.ascontiguousarray(inp["positions"][b], dtype=np.int32)
        in_maps.append(m)
    res = run_bass_kernel_spmd(nc, in_maps, core_ids=list(range(B)))
    return np.stack([np.asarray(r["y"], dtype=np.float32) for r in res.results], axis=0)
```

```python
from contextlib import ExitStack
from concourse.bass_utils import run_bass_kernel_spmd
import numpy as np
import concourse.bass as bass
import concourse.mybir as mybir

F32 = mybir.dt.float32
BF16 = mybir.dt.bfloat16
I32 = mybir.dt.int32
AF = mybir.ActivationFunctionType
ALU = mybir.AluOpType
AX = mybir.AxisListType

_DSZ = {F32: 4, BF16: 2, I32: 4, mybir.dt.float32r: 4, mybir.dt.uint32: 4,
        mybir.dt.float16: 2, mybir.dt.uint16: 2, mybir.dt.int16: 2, mybir.dt.uint8: 1, mybir.dt.int8: 1}

FAST_RECIP = True
ENGS = ("pe", "act", "dve", "pool", "sp")
COMPUTE = ("pe", "act", "dve", "pool")


def region(ap):
    sz = _DSZ[ap.dtype]
    pat = list(ap.ap)
    space = str(ap.space)
    if space == "DRAM":
        ext = sum((c - 1) * abs(s) for s, c in pat) + 1
        W = int(ap.tensor.shape[-1])
        c0, c1 = 0, W
        ls, ln = pat[-1]
        if ls == 1 and ln <= W and all(s % W == 0 for s, c in pat[:-1]) and (ap.offset % W) + ln <= W:
            c0 = ap.offset % W
            c1 = c0 + ln
        return (("D", ap.name), 0, 1, ap.offset * sz, (ap.offset + ext) * sz, c0, c1)
    if space == "PSUM":
        return ((space, ap.name), 0, 128, 0, 1 << 30, 0, 1 << 40)
    pstride, pcnt = pat[0]
    if pstride == 0:
        pstride = 1 << 40
    p0 = ap.offset // pstride if pstride < (1 << 40) else 0
    col = ap.offset - p0 * pstride if pstride < (1 << 40) else ap.offset
    ext = sum((c - 1) * abs(s) for s, c in pat[1:]) + 1
    return ((space, ap.name), p0, p0 + pcnt, col * sz, (col + ext) * sz, 0, 1 << 40)


def _ov(a, b):
    return a[1] < b[2] and b[1] < a[2] and a[3] < b[4] and b[3] < a[4] and a[5] < b[6] and b[5] < a[6]


def _cov(a, b):
    return a[1] <= b[1] and a[2] >= b[2] and a[3] <= b[3] and a[4] >= b[4] and a[5] <= b[5] and a[6] >= b[6]


class Op:
    __slots__ = ("eng", "fn", "deps", "dma", "idx", "inc", "tok", "waits")

    def __init__(self, eng, fn, dma):
        self.eng, self.fn, self.dma = eng, fn, dma
        self.deps = []
        self.inc = False
        self.tok = None
        self.waits = []


class Sched:
    def __init__(self, nc, sems, ndma=24):
        self.nc = nc
        self.sems = sems
        self.cnt = {e: 0 for e in COMPUTE}
        self.dma_val = {}
        self.dma_rr = {e: 0 for e in ENGS}
        self.reset()

    def reset(self):
        self.ops = {e: [] for e in ENGS}
        self.wr = {}
        self.rd = {}

    def add(self, eng, fn, reads=(), writes=(), dma=False):
        op = Op(eng, fn, dma)
        op.idx = len(self.ops[eng])
        reads = list(reads)
        writes = list(writes) + [a for a in reads if str(a.space) == "PSUM"]
        deps = []
        for ap in reads:
            r = region(ap)
            for (rr, o) in self.wr.get(r[0], ()):
                if _ov(r, rr):
                    deps.append((o, "raw"))
        wregs = []
        for ap in writes:
            r = region(ap)
            wregs.append(r)
            for (rr, o) in self.wr.get(r[0], ()):
                if _ov(r, rr):
                    deps.append((o, "waw"))
            for (rr, o) in self.rd.get(r[0], ()):
                if _ov(r, rr):
                    deps.append((o, "war"))
        op.deps = deps
        for ap in reads:
            r = region(ap)
            lst = self.rd.setdefault(r[0], [])
            if not dma:
                for i, (rr, o) in enumerate(lst):
                    if rr == r and o.eng == eng and not o.dma:
                        lst[i] = (r, op)
                        break
                else:
                    lst.append((r, op))
            else:
                lst.append((r, op))
        for r in wregs:
            k = r[0]
            self.wr[k] = [(rr, o) for (rr, o) in self.wr.get(k, ()) if not _cov(r, rr)] + [(r, op)]
            if k in self.rd:
                self.rd[k] = [(rr, o) for (rr, o) in self.rd[k] if not _cov(r, rr) or o is op]
        self.ops[eng].append(op)
        return op

    def emit_block(self, wait_all_dma=True):
        nc = self.nc
        self._assign_tokens()
        val = self._flag_and_count()
        self._compute_waits(val)
        engmap = {"pe": "tensor", "act": "scalar", "dve": "vector", "pool": "gpsimd", "sp": "sync"}
        with nc.Block() as block:
            for e in ENGS:
                ops = self.ops[e]
                tail = []
                if wait_all_dma:
                    for s in self.sems["dma"][e]:
                        v = self.dma_val.get(id(s), 0)
                        if v > 0:
                            tail.append((s, v))
                if not ops and not tail:
                    continue

                def body(eng, ops=ops, e=e, tail=tail):
                    for op in ops:
                        for (s, v) in op.waits:
                            eng.wait_ge(s, v)
                        inst = op.fn(eng)
                        if op.dma:
                            inst.then_inc(op.tok[0], 16)
                        elif op.inc:
                            inst.then_inc(self.sems[e], 1)
                    for (s, v) in tail:
                        eng.wait_ge(s, v)

                getattr(block, engmap[e])(body)
        self.reset()

    def _assign_tokens(self):
        self._reuse_wait = {}
        for e in ENGS:
            for op in self.ops[e]:
                if op.dma:
                    pool = self.sems["dma"][e]
                    s = pool[self.dma_rr[e] % len(pool)]
                    self.dma_rr[e] += 1
                    prev = self.dma_val.get(id(s), 0)
                    self.dma_val[id(s)] = prev + 16
                    op.tok = (s, prev + 16)
                    self._reuse_wait[id(op)] = (s, prev)

    def _needs_edge(self, o, op, kind):
        if o.dma:
            return True
        if o.eng == op.eng and not op.dma:
            if o.eng == "pe" or kind != "raw":
                return False
        return True

    def _flag_and_count(self):
        for e in ENGS:
            for op in self.ops[e]:
                for (o, kind) in op.deps:
                    if not o.dma and self._needs_edge(o, op, kind):
                        o.inc = True
        val = {}
        for e in COMPUTE:
            c = self.cnt[e]
            for op in self.ops[e]:
                if not op.dma and op.inc:
                    c += 1
                    val[id(op)] = c
            self.cnt[e] = c
        return val

    def _compute_waits(self, val):
        for e in ENGS:
            waited = {}
            for op in self.ops[e]:
                need = {}
                if op.dma:
                    s, prev = self._reuse_wait[id(op)]
                    if prev > 0:
                        need[id(s)] = (s, prev)
                for (o, kind) in op.deps:
                    if not self._needs_edge(o, op, kind):
                        continue
                    if o.dma:
                        s, v = o.tok
                    else:
                        s, v = self.sems[o.eng], val[id(o)]
                    if need.get(id(s), (None, 0))[1] < v:
                        need[id(s)] = (s, v)
                op.waits = []
                for k, (s, v) in need.items():
                    if waited.get(k, 0) < v:
                        op.waits.append((s, v))
                        waited[k] = v


class Ctx:
    def __init__(self, nc, S):
        self.nc = nc
        self.S = S
        self.stack = ExitStack()
        sems = {}
        for e in COMPUTE:
            sems[e] = self.stack.enter_context(nc.semaphore("c_" + e))
        sems["dma"] = {}
        for e, n in (("sp", 24), ("pool", 12), ("act", 8), ("dve", 2), ("pe", 2)):
            sems["dma"][e] = [self.stack.enter_context(nc.semaphore("d_%s%d" % (e, i))) for i in range(n)]
        self.sch = Sched(nc, sems)
        self.uid = 0

    def phase(self):
        return Phase(self)


class Phase:
    def __init__(self, ctx):
        self.c = ctx
        self.nc = ctx.nc
        self.s = ctx.sch
        self.st = ExitStack()

    def __enter__(self):
        self.st.__enter__()
        return self

    def __exit__(self, *a):
        if a[0] is None:
            self.s.emit_block()
        return self.st.__exit__(*a)

    def sb(self, name, shape, dt=F32):
        self.c.uid += 1
        return self.st.enter_context(self.nc.sbuf_tensor("%s_%d" % (name, self.c.uid), list(shape), dt))

    def ps(self, name, shape=(128, 512), dt=F32):
        self.c.uid += 1
        return self.st.enter_context(self.nc.psum_tensor("%s_%d" % (name, self.c.uid), list(shape), dt))

    def mm(self, out, lhsT, rhs, start=True, stop=True):
        self.s.add("pe", lambda e: e.matmul(out, lhsT, rhs, start=start, stop=stop),
                   reads=[lhsT, rhs], writes=[out])

    def tr(self, out, in_, ident):
        self.s.add("pe", lambda e: e.transpose(out, in_, ident), reads=[in_, ident], writes=[out])

    def act(self, out, in_, func, bias=None, scale=None, accum=None, eng="act"):
        kw = {}
        rd = [in_]
        wr = [out]
        if bias is not None:
            kw["bias"] = bias
            if not isinstance(bias, (int, float)):
                rd.append(bias)
        if scale is not None:
            kw["scale"] = scale
            if not isinstance(scale, (int, float)):
                rd.append(scale)
        if accum is not None:
            kw["accum_out"] = accum
            wr.append(accum)
        self.s.add("act", lambda e: e.activation(out, in_, func, **kw), reads=rd, writes=wr)

    def tt(self, eng, out, in0, in1, op):
        self.s.add(eng, lambda e: e.tensor_tensor(out, in0, in1, op), reads=[in0, in1], writes=[out])

    def ts(self, eng, out, in0, s1, op0, s2=None, op1=None, accum=None):
        rd = [in0] + [x for x in (s1, s2) if x is not None and not isinstance(x, (int, float))]
        wr = [out] + ([accum] if accum is not None else [])
        kw = {}
        if op1 is not None:
            kw["op1"] = op1
        if accum is not None:
            kw["accum_out"] = accum
        self.s.add(eng, lambda e: e.tensor_scalar(out, in0, s1, s2, op0, **kw), reads=rd, writes=wr)

    def stt(self, out, in0, scalar, in1, op0, op1, eng="dve"):
        rd = [in0, in1] + ([scalar] if not isinstance(scalar, (int, float)) else [])
        self.s.add(eng, lambda e: e.scalar_tensor_tensor(out, in0, scalar, in1, op0, op1), reads=rd, writes=[out])

    def copy(self, eng, out, in_):
        if eng == "act":
            self.s.add("act", lambda e: e.copy(out, in_), reads=[in_], writes=[out])
        else:
            self.s.add(eng, lambda e: e.tensor_copy(out, in_), reads=[in_], writes=[out])

    def recip(self, out, in_):
        if int(np.prod(out.shape[1:])) >= 64:
            self.act(out, in_, AF.Ln)
            self.act(out, out, AF.Exp, scale=-1.0)
        else:
            self.s.add("dve", lambda e: e.reciprocal(out, in_), reads=[in_], writes=[out])

    def memset(self, eng, ap, v):
        self.s.add(eng, lambda e: e.memset(ap, v), reads=[], writes=[ap])

    def dma(self, eng, out, in_):
        self.s.add(eng, lambda e: e.dma_start(out=out, in_=in_), reads=[in_], writes=[out], dma=True)

    def rstd(self, out, ss, inv_n, eps=1e-6):
        if int(np.prod(out.shape[1:])) >= 64:
            self.act(out, ss, AF.Ln, scale=float(inv_n), bias=self.epsb(eps))
            self.act(out, out, AF.Exp, scale=-0.5)
        else:
            self.ts("dve", out, ss, inv_n, ALU.mult, eps, ALU.add)
            self.act(out, out, AF.Sqrt)
            self.s.add("dve", lambda e: e.reciprocal(out, out), reads=[out], writes=[out])

    def epsb(self, eps):
        if getattr(self, "_epsb", None) is None:
            self._epsb = self.sb("epsb", [128, 1])
            self.memset("dve", self._epsb[:], float(eps))
        return self._epsb[:]

D = 1024
DEPTH = 4
D_FF = 2816
N_MEM = 256
IN_COLS = 2736
FFN1_PRE, FFN1_POST, MIX_PRE, MIX_POST, MEM_NORM, XA_PRE, XA_POST, FFN2_PRE, FFN2_POST = range(9)

PP_NORM = 0
PP_SCW = 72
PP_SCB = 96
PP_GCW = 102
PP_QG = 126
PP_KVG = 128
PP_SDTB = 129
PP_SALOG = 137
PP_SD = 145
PP_GDTB = 153
PP_GALOG = 157
PP_SNG = 161
PP_GNG = 673
NPP = 737


DEBUG = {}


def ffn_phase(ctx, W, L, which, HT_in, HT_out):
    S = ctx.S
    TS = min(1024, S)
    NSUP = S // TS
    NSUB = TS // 512
    w_up = W["ffn_w_up"][L, which]
    w_dn = W["ffn_w_down"][L, which]
    pre = (FFN1_PRE, FFN2_PRE)[which]
    post = (FFN1_POST, FFN2_POST)[which]
    HTi = HT_in.rearrange("(c p) t -> p c t", p=128)
    HTo = HT_out.rearrange("(c p) t -> p c t", p=128)
    with ctx.phase() as p:
        pp = p.sb("pp", [128, NPP])
        p.dma("sp", pp[:], W["pp"][L])
        ones = p.sb("ones", [128, 128], BF16)
        p.memset("dve", ones[:], 1.0)
        ghalf = p.sb("gh", [128, 8])
        p.ts("dve", ghalf[:], pp[:, post * 8:(post + 1) * 8], 0.5, ALU.mult)
        wdn = p.sb("wdn", [128, 22, 1024], BF16)
        hbuf = [p.sb("hb%d" % i, [128, 8, 512]) for i in range(2)]
        uTs = [p.sb("uT%d" % i, [128, 8, TS], BF16) for i in range(2)]
        hid = p.sb("hid", [128, 22, TS], BF16)
        wup = [p.sb("wup%d" % i, [128, 8, 512], BF16) for i in range(2)]
        ff = p.sb("ff", [128, 8, 512])
        sq = p.sb("sq", [128, 8, 512], BF16)
        rbc = p.sb("rbc", [128, 512])
        sl = [p.sb("sl%d" % i, [128, 512], BF16) for i in range(2)]
        pg = [p.ps("pg%d" % i) for i in range(2)]
        pu = [p.ps("pu%d" % i) for i in range(2)]
        pss = p.ps("pss")
        pd = [p.ps("pd%d" % i) for i in range(2)]
        st = {"hb": 0}

        def next_hb():
            hb = hbuf[st["hb"] % 2]
            st["hb"] += 1
            return hb

        def stage_a(su):
            uT = uTs[su % 2]
            for s_ in range(NSUB):
                c0 = su * TS + s_ * 512
                hb = next_hb()
                p.dma("sp", hb[:], HTi[:, :, c0:c0 + 512])
                prenorm(p, hb, sq, ones, pss, rbc, pp[:, pre * 8:pre * 8 + 8],
                        lambda kc, uT=uT, s_=s_: uT[:, kc, s_ * 512:(s_ + 1) * 512])

        def stage_b(su):
            uT = uTs[su % 2]
            for jp in range(11):
                wb = wup[jp % 2]
                p.dma("pool", wb[:, :, 0:256],
                      w_up[:, jp * 256:(jp + 1) * 256].rearrange("(c p) f -> p c f", p=128))
                p.dma("pool", wb[:, :, 256:512],
                      w_up[:, D_FF + jp * 256:D_FF + (jp + 1) * 256].rearrange("(c p) f -> p c f", p=128))
                if su == 0 and jp == 1:
                    for half in range(2):
                        p.dma("pool", wdn[:, half * 11:(half + 1) * 11, :],
                              w_dn[half * 1408:(half + 1) * 1408, :].rearrange("(c p) f -> p c f", p=128))
                for jj in range(2):
                    j = jp * 2 + jj
                    for s_ in range(NSUB):
                        cs = slice(s_ * 512, (s_ + 1) * 512)
                        gps, ups = pg[s_ % 2], pu[s_ % 2]
                        for kc in range(8):
                            p.mm(gps[:], wb[:, kc, jj * 128:(jj + 1) * 128], uT[:, kc, cs], start=(kc == 0), stop=(kc == 7))
                        for kc in range(8):
                            p.mm(ups[:], wb[:, kc, 256 + jj * 128:256 + (jj + 1) * 128], uT[:, kc, cs], start=(kc == 0), stop=(kc == 7))
                        p.act(sl[s_ % 2][:], gps[:], AF.Silu)
                        p.tt("dve", hid[:, j, cs], sl[s_ % 2][:], ups[:], ALU.mult)

        def stage_c(su):
            for s_ in range(NSUB):
                c0 = su * TS + s_ * 512
                cs = slice(s_ * 512, (s_ + 1) * 512)
                hb = next_hb()
                p.dma("sp", hb[:], HTi[:, :, c0:c0 + 512])
                proj_post(p, wdn, 22, lambda j, cs=cs: hid[:, j, cs], pd, ff, sq, ones, pss, rbc, ghalf, hb)
                p.dma("sp", HTo[:, :, c0:c0 + 512], hb[:])

        stage_a(0)
        for su in range(NSUP):
            stage_b(su)
            if su + 1 < NSUP:
                stage_a(su + 1)
            stage_c(su)


def to_feature_major(ctx, W, x, HT):
    S = ctx.S
    HTv = HT.rearrange("(c p) t -> p c t", p=128)
    with ctx.phase() as p:
        ident = p.sb("ident", [128, 128])
        p.dma("sp", ident[:], W["ident"])
        xin = [p.sb("xin%d" % i, [128, 1024]) for i in range(2)]
        xo = [p.sb("xo%d" % i, [128, 8, 512]) for i in range(2)]
        pt = [p.ps("pt%d" % i) for i in range(4)]
        for g in range(S // 512):
            o = xo[g % 2]
            for tt in range(4):
                t0 = g * 512 + tt * 128
                xi = xin[tt % 2]
                p.dma("sp", xi[:], x[t0:t0 + 128, :])
                for hf in range(2):
                    ptt = pt[(tt % 2) * 2 + hf]
                    for q in range(4):
                        kc = hf * 4 + q
                        p.tr(ptt[:, q * 128:(q + 1) * 128], xi[:, kc * 128:(kc + 1) * 128], ident[:])
                    dst = o[:, hf * 4:(hf + 1) * 4, tt * 128:(tt + 1) * 128]
                    p.copy("dve" if hf == 0 else "act", dst, ptt[:].rearrange("p (q t) -> p q t", q=4))
            p.dma("sp", HTv[:, :, g * 512:(g + 1) * 512], o[:])


def to_token_major(ctx, W, HT, y):
    S = ctx.S
    HTv = HT.rearrange("(c p) t -> p c t", p=128)
    with ctx.phase() as p:
        ident = p.sb("ident", [128, 128])
        p.dma("sp", ident[:], W["ident"])
        hin = [p.sb("hin%d" % i, [128, 8, 512]) for i in range(2)]
        yo = [p.sb("yo%d" % i, [128, 1024]) for i in range(2)]
        pt = [p.ps("pt%d" % i) for i in range(4)]
        for g in range(S // 512):
            hi = hin[g % 2]
            p.dma("sp", hi[:], HTv[:, :, g * 512:(g + 1) * 512])
            for tt in range(4):
                t0 = g * 512 + tt * 128
                o = yo[tt % 2]
                for hf in range(2):
                    ptt = pt[(tt % 2) * 2 + hf]
                    for q in range(4):
                        kc = hf * 4 + q
                        p.tr(ptt[:, q * 128:(q + 1) * 128], hi[:, kc, tt * 128:(tt + 1) * 128], ident[:])
                    p.copy("dve" if hf == 0 else "act", o[:, hf * 512:(hf + 1) * 512], ptt[:])
                p.dma("sp", y[t0:t0 + 128, :], o[:])


def make_pp(inp, depth):
    pp = np.zeros((depth, 128, NPP), np.float32)
    for l in range(depth):
        ng = np.asarray(inp["norm_g"][l], np.float32)
        pp[l, :, PP_NORM:PP_NORM + 72] = ng.reshape(9, 8, 128).transpose(2, 0, 1).reshape(128, 72)
        cw = np.asarray(inp["ssd_conv_w"][l], np.float32)
        pp[l, :, PP_SCW:PP_SCW + 24] = cw.reshape(4, 6, 128).transpose(2, 1, 0).reshape(128, 24)
        pp[l, :, PP_SCB:PP_SCB + 6] = np.asarray(inp["ssd_conv_b"][l], np.float32).reshape(6, 128).T
        gw = np.asarray(inp["gdn_conv_w"][l], np.float32)
        pp[l, :, PP_GCW:PP_GCW + 24] = gw.reshape(4, 6, 128).transpose(2, 1, 0).reshape(128, 24)
        pp[l, :, PP_QG:PP_QG + 2] = np.asarray(inp["mla_q_norm_g"][l], np.float32).reshape(2, 128).T
        pp[l, :, PP_KVG:PP_KVG + 1] = np.asarray(inp["mla_kv_norm_g"][l], np.float32).reshape(1, 128).T
        for off, key, n in ((PP_SDTB, "ssd_dt_bias", 8), (PP_SALOG, "ssd_a_log", 8), (PP_SD, "ssd_d", 8),
                            (PP_GDTB, "gdn_dt_bias", 4), (PP_GALOG, "gdn_a_log", 4),
                            (PP_SNG, "ssd_norm_g", 512), (PP_GNG, "gdn_norm_g", 64)):
            pp[l, :, off:off + n] = np.broadcast_to(np.asarray(inp[key][l], np.float32)[None, :], (128, n))
    return pp


def prenorm(p, hb, sq, ones, pss, rbc, gcols, dst):
    p.act(sq[:], hb[:], AF.Square)
    for kc in range(8):
        p.mm(pss[:], ones[:], sq[:, kc, :], start=(kc == 0), stop=(kc == 7))
    p.rstd(rbc[:], pss[:], 1.0 / D)
    for kc in range(8):
        p.stt(dst(kc), hb[:, kc, :], gcols[:, kc:kc + 1], rbc[:], ALU.mult, ALU.mult)


def proj_post(p, wt, KC, rhs, pd, ff, sq, ones, pss, rbc, gpost, hb):
    for fo in range(8):
        pdd = pd[fo % 2]
        for kc in range(KC):
            p.mm(pdd[:], wt[:, kc, fo * 128:(fo + 1) * 128], rhs(kc), start=(kc == 0), stop=(kc == KC - 1))
        p.copy("dve", ff[:, fo, :], pdd[:])
        p.act(sq[:, fo, :], ff[:, fo, :], AF.Square)
    for fo in range(8):
        p.mm(pss[:], ones[:], sq[:, fo, :], start=(fo == 0), stop=(fo == 7))
    p.rstd(rbc[:], pss[:], 1.0 / D)
    for kc in range(8):
        p.stt(ff[:, kc, :], ff[:, kc, :], gpost[:, kc:kc + 1], rbc[:], ALU.mult, ALU.mult)
        p.tt("dve", hb[:, kc, :], hb[:, kc, :], ff[:, kc, :], ALU.add)


def load_w_bf16(p, dst, src, kc_n, ncols, colchunk=1024):
    v = src.rearrange("(c p) f -> p c f", p=128)
    for c0 in range(0, ncols, colchunk):
        c1 = min(ncols, c0 + colchunk)
        p.dma("pool", dst[:, :, c0:c1], v[:, :, c0:c1])


CF_XBC, CF_QKV, CF_CQ, CF_CKV, CF_KR, CF_ROWS = 0, 768, 1536, 1792, 1920, 1952
CT_ZS, CT_ZG, CT_COLS = 0, 512, 768
CS_B, CS_A, CS_DT, CS_COLS = 0, 4, 8, 16


def inproj_phase(ctx, W, L, HT, COLF, COLT, COLS):
    S = ctx.S
    HTv = HT.rearrange("(c p) t -> p c t", p=128)
    w_in = W["w_in"][L]
    fm = [(512 + i * 128, 128, CF_XBC + i * 128) for i in range(6)] + \
         [(1704 + i * 128, 128, CF_QKV + i * 128) for i in range(6)] + \
         [(1288, 128, CF_CQ), (1416, 128, CF_CQ + 128), (1544, 128, CF_CKV), (1672, 32, CF_KR)]
    with ctx.phase() as p:
        pp = p.sb("pp", [128, NPP])
        p.dma("sp", pp[:], W["pp"][L])
        ones = p.sb("ones", [128, 128], BF16)
        p.memset("dve", ones[:], 1.0)
        wi = p.sb("wi", [128, 8, IN_COLS], BF16)
        load_w_bf16(p, wi, w_in, 8, IN_COLS, 1368)
        hbuf = [p.sb("hb%d" % i, [128, 8, 512]) for i in range(2)]
        uT = p.sb("uT", [128, 8, 512], BF16)
        sq = p.sb("sq", [128, 8, 512], BF16)
        rbc = p.sb("rbc", [128, 512])
        cf = [p.sb("cf%d" % i, [128, 16, 512]) for i in range(2)]
        ct = [p.sb("ct%d" % i, [128, 784]) for i in range(2)]
        pss = p.ps("pss")
        pf = [p.ps("pf%d" % i) for i in range(3)]
        pt = [p.ps("pt%d" % i) for i in range(4)]
        for s_ in range(S // 512):
            c0 = s_ * 512
            hb = hbuf[s_ % 2]
            p.dma("sp", hb[:], HTv[:, :, c0:c0 + 512])
            prenorm(p, hb, sq, ones, pss, rbc, pp[:, PP_NORM + MIX_PRE * 8:PP_NORM + MIX_PRE * 8 + 8],
                    lambda kc: uT[:, kc, :])
            cfb = cf[s_ % 2]
            for i, (wc, m, row) in enumerate(fm):
                ps_ = pf[i % 3]
                for kc in range(8):
                    p.mm(ps_[0:m, :], wi[:, kc, wc:wc + m], uT[:, kc, :], start=(kc == 0), stop=(kc == 7))
                p.copy("act" if i % 2 == 0 else "dve", cfb[0:m, i, :], ps_[0:m, :])
            p.dma("pool", COLF[0:1920, c0:c0 + 512].rearrange("(c p) t -> p c t", p=128), cfb[:, 0:15, :])
            p.dma("pool", COLF[1920:1952, c0:c0 + 512], cfb[0:32, 15, :])
            for tt in range(4):
                ts_ = slice(tt * 128, (tt + 1) * 128)
                pa, pb = pt[(tt % 2) * 2], pt[(tt % 2) * 2 + 1]
                ctb = ct[tt % 2]
                for kc in range(8):
                    p.mm(pa[:], uT[:, kc, ts_], wi[:, kc, 0:512], start=(kc == 0), stop=(kc == 7))
                for kc in range(8):
                    p.mm(pb[:, 0:264], uT[:, kc, ts_], wi[:, kc, 2472:2736], start=(kc == 0), stop=(kc == 7))
                for kc in range(8):
                    p.mm(pb[:, 264:272], uT[:, kc, ts_], wi[:, kc, 1280:1288], start=(kc == 0), stop=(kc == 7))
                p.copy("act", ctb[:, 0:512], pa[:])
                p.copy("dve", ctb[:, 512:784], pb[:, 0:272])
                p.dma("pool", COLT[c0 + tt * 128:c0 + (tt + 1) * 128, :], ctb[:, 0:768])
                p.dma("pool", COLS[c0 + tt * 128:c0 + (tt + 1) * 128, :], ctb[:, 768:784])


def outproj_phase(ctx, W, L, HT, YT):
    S = ctx.S
    HTv = HT.rearrange("(c p) t -> p c t", p=128)
    YTv = YT.rearrange("(c p) t -> p c t", p=128)
    with ctx.phase() as p:
        pp = p.sb("pp", [128, NPP])
        p.dma("sp", pp[:], W["pp"][L])
        ones = p.sb("ones", [128, 128], BF16)
        p.memset("dve", ones[:], 1.0)
        wo = p.sb("wo", [128, 8, 1024], BF16)
        load_w_bf16(p, wo, W["w_out"][L], 8, 1024)
        hbuf = [p.sb("hb%d" % i, [128, 8, 512]) for i in range(2)]
        yb = [p.sb("yb%d" % i, [128, 8, 512], BF16) for i in range(2)]
        ff = p.sb("ff", [128, 8, 512])
        sq = p.sb("sq", [128, 8, 512], BF16)
        rbc = p.sb("rbc", [128, 512])
        pss = p.ps("pss")
        pd = [p.ps("pd%d" % i) for i in range(2)]
        gpost = pp[:, PP_NORM + MIX_POST * 8:PP_NORM + MIX_POST * 8 + 8]
        for s_ in range(S // 512):
            c0 = s_ * 512
            hb, y = hbuf[s_ % 2], yb[s_ % 2]
            p.dma("sp", hb[:], HTv[:, :, c0:c0 + 512])
            p.dma("sp", y[:], YTv[:, :, c0:c0 + 512])
            proj_post(p, wo, 8, lambda kc: y[:, kc, :], pd, ff, sq, ones, pss, rbc, gpost, hb)
            p.dma("pool", HTv[:, :, c0:c0 + 512], hb[:])


def xattn_phase(ctx, W, L, HT, mem):
    S = ctx.S
    HTv = HT.rearrange("(c p) t -> p c t", p=128)
    with ctx.phase() as p:
        pp = p.sb("pp", [128, NPP])
        p.dma("sp", pp[:], W["pp"][L])
        ident = p.sb("ident", [128, 128])
        p.dma("sp", ident[:], W["ident"])
        ones = p.sb("ones", [128, 128], BF16)
        p.memset("dve", ones[:], 1.0)
        wkv = p.sb("wkv", [128, 8, 2048], BF16)
        load_w_bf16(p, wkv, W["xa_w_kv"][L], 8, 2048)
        wq = p.sb("wq", [128, 8, 1024], BF16)
        load_w_bf16(p, wq, W["xa_w_q"][L], 8, 1024)
        wo = p.sb("wo", [128, 8, 1024], BF16)
        load_w_bf16(p, wo, W["xa_w_o"][L], 8, 1024)
        memT = p.sb("memT", [128, 8, 256], BF16)
        kT = p.sb("kT", [128, 8, 256], BF16)
        v = p.sb("v", [128, 2, 1024], BF16)
        hbuf = [p.sb("hb%d" % i, [128, 8, 512]) for i in range(2)]
        uT = p.sb("uT", [128, 8, 512], BF16)
        qT = p.sb("qT", [128, 8, 512], BF16)
        oT = p.sb("oT", [128, 8, 512], BF16)
        pT = [p.sb("pT%d" % i, [128, 512], BF16) for i in range(4)]
        rden = p.sb("rden", [128, 512])
        ff = p.sb("ff", [128, 8, 512])
        sq = p.sb("sq", [128, 8, 512], BF16)
        rbc = p.sb("rbc", [128, 512])
        mt_ = p.sb("mt", [128, 1024])
        st = p.sb("st", [128, 4])
        junk = p.sb("junk", [128, 1024])
        pss = p.ps("pss")
        pd = [p.ps("pd%d" % i) for i in range(2)]
        pq = [p.ps("pq%d" % i) for i in range(2)]
        pden = p.ps("pden")
        po = [p.ps("po%d" % i) for i in range(2)]
        gmem = pp[:, PP_NORM + MEM_NORM * 8:PP_NORM + MEM_NORM * 8 + 8]
        XM = DEBUG.get('XM', 9)
        for mt in range(2):
            p.dma("sp", mt_[:], mem[mt * 128:(mt + 1) * 128, :])
            if XM < 1:
                continue
            p.act(junk[:], mt_[:], AF.Square, accum=st[:, 0:1])
            p.rstd(st[:, 1:2], st[:, 0:1], 1.0 / D)
            p.ts("dve", mt_[:], mt_[:], st[:, 1:2], ALU.mult)
            for hf in range(2):
                if XM < 2:
                    break
                ptt = pq[hf]
                for q in range(4):
                    kc = hf * 4 + q
                    p.tr(ptt[:, q * 128:(q + 1) * 128], mt_[:, kc * 128:(kc + 1) * 128], ident[:])
                for q in range(4):
                    if XM < 3:
                        break
                    kc = hf * 4 + q
                    p.ts("dve", memT[:, kc, mt * 128:(mt + 1) * 128], ptt[:, q * 128:(q + 1) * 128],
                         gmem[:, kc:kc + 1], ALU.mult)
        for dc in range(8):
            if XM < 4:
                break
            ps_ = pq[dc % 2]
            for kc in range(8):
                p.mm(ps_[:, 0:256], wkv[:, kc, dc * 128:(dc + 1) * 128], memT[:, kc, :],
                     start=(kc == 0), stop=(kc == 7))
            p.copy("act" if dc % 2 == 0 else "dve", kT[:, dc, :], ps_[:, 0:256])
        for mt in range(2):
            if XM < 5:
                break
            for hf in range(2):
                ps_ = po[hf]
                for kc in range(8):
                    p.mm(ps_[:], memT[:, kc, mt * 128:(mt + 1) * 128], wkv[:, kc, 1024 + hf * 512:1024 + (hf + 1) * 512],
                         start=(kc == 0), stop=(kc == 7))
                p.copy("act" if hf == 0 else "dve", v[:, mt, hf * 512:(hf + 1) * 512], ps_[:])
        gpre = pp[:, PP_NORM + XA_PRE * 8:PP_NORM + XA_PRE * 8 + 8]
        gpost = pp[:, PP_NORM + XA_POST * 8:PP_NORM + XA_POST * 8 + 8]
        for s_ in range(S // 512):
            if DEBUG.get('XA', 9) < 1:
                break
            c0 = s_ * 512
            hb = hbuf[s_ % 2]
            p.dma("sp", hb[:], HTv[:, :, c0:c0 + 512])
            prenorm(p, hb, sq, ones, pss, rbc, gpre, lambda kc: uT[:, kc, :])
            for dc in range(8):
                ps_ = pq[dc % 2]
                for kc in range(8):
                    p.mm(ps_[:], wq[:, kc, dc * 128:(dc + 1) * 128], uT[:, kc, :], start=(kc == 0), stop=(kc == 7))
                p.act(qT[:, dc, :], ps_[:], AF.Copy, scale=1.0 / 16.0)
            def xa_sc(h):
                for mt in range(2):
                    ps_ = pq[mt]
                    for dc in range(2):
                        p.mm(ps_[:], kT[:, 2 * h + dc, mt * 128:(mt + 1) * 128], qT[:, 2 * h + dc, :],
                             start=(dc == 0), stop=(dc == 1))
                    p.act(pT[(h % 2) * 2 + mt][:], ps_[:], AF.Exp)

            def xa_av(h):
                for mt in range(2):
                    p.mm(pden[:], ones[:], pT[(h % 2) * 2 + mt][:], start=(mt == 0), stop=(mt == 1))
                p.recip(rden[:], pden[:])
                for dc in range(2):
                    ps_ = po[dc]
                    for mt in range(2):
                        p.mm(ps_[:], v[:, mt, (2 * h + dc) * 128:(2 * h + dc + 1) * 128], pT[(h % 2) * 2 + mt][:],
                             start=(mt == 0), stop=(mt == 1))
                    p.tt("dve", oT[:, 2 * h + dc, :], ps_[:], rden[:], ALU.mult)

            xa_sc(0)
            for h in range(4):
                if h < 3:
                    xa_sc(h + 1)
                xa_av(h)
            proj_post(p, wo, 8, lambda kc: oT[:, kc, :], pd, ff, sq, ones, pss, rbc, gpost, hb)
            p.dma("pool", HTv[:, :, c0:c0 + 512], hb[:])


C_TRI, C_BLK, C_SUP, C_SEL, C_IDB, C_SLO, C_MKQ, C_INVF, C_LINC, NCST = 0, 128, 256, 384, 896, 1024, 1152, 1280, 1281, 1409


def make_cst():
    j = np.arange(128)[:, None]
    l = np.arange(128)[None, :]
    same = (j // 64) == (l // 64)
    c = np.zeros((128, NCST), np.float32)
    c[:, C_TRI:C_TRI + 128] = (same & (j <= l))
    c[:, C_BLK:C_BLK + 128] = same
    c[:, C_SUP:C_SUP + 128] = (same & (j > l))
    for ch in range(2):
        for g in range(2):
            blk = np.zeros((128, 128), np.float32)
            blk[ch * 64 + 63, g * 64:(g + 1) * 64] = 1.0
            c[:, C_SEL + (ch * 2 + g) * 128:C_SEL + (ch * 2 + g + 1) * 128] = blk
    c[:, C_IDB:C_IDB + 128] = np.eye(128)
    c[:, C_SLO:C_SLO + 128] = (same & (j < l))
    c[:, C_MKQ:C_MKQ + 128] = ((j // 64) <= (l // 64))
    c[:, C_LINC:C_LINC + 128] = (same & (l <= j))
    inv = 1.0 / (10000.0 ** (np.arange(0, 32, 2, dtype=np.float32) / 32.0))
    c[0:32, C_INVF] = np.concatenate([inv, inv]) / (2.0 * np.pi)
    return c


def softplus_small(p, out, x, tmp):
    p.stt(tmp, x, -1.0, x, ALU.mult, ALU.max)
    p.act(tmp, tmp, AF.Exp, scale=-1.0)
    p.act(tmp, tmp, AF.Ln, bias=1.0)
    p.stt(out, x, 0.0, tmp, ALU.max, ALU.add)


def ssd_phase(ctx, W, L, COLF, COLT, COLS, YT):
    S = ctx.S
    NT = S // 128
    with ctx.phase() as p:
        pp = p.sb("pp", [128, NPP])
        p.dma("sp", pp[:], W["pp"][L])
        ident = p.sb("ident", [128, 128])
        p.dma("sp", ident[:], W["ident"])
        cst = p.sb("cst", [128, NCST])
        p.dma("sp", cst[:], W["cst"])
        TRI = cst[:, C_TRI:C_TRI + 128]
        BLK = cst[:, C_BLK:C_BLK + 128]
        SUP = cst[:, C_SUP:C_SUP + 128]
        identb = p.sb("identb", [128, 128], BF16)
        p.copy("dve", identb[:], cst[:, C_IDB:C_IDB + 128])
        tri3 = p.sb("tri3", [128, 1, 128])
        p.copy("dve", tri3[:, 0, :], TRI)
        a_bc = p.sb("a_bc", [128, 1, 8])
        p.act(a_bc[:, 0, :], pp[:, PP_SALOG:PP_SALOG + 8], AF.Exp)
        p.ts("dve", a_bc[:, 0, :], a_bc[:, 0, :], -1.0, ALU.mult)
        dsk = p.sb("dsk", [128, 8, 1])
        p.copy("dve", dsk[:, :, 0], pp[:, PP_SD:PP_SD + 8])
        dtb = p.sb("dtb", [128, 1, 8])
        p.copy("dve", dtb[:, 0, :], pp[:, PP_SDTB:PP_SDTB + 8])
        dtr = p.sb("dtr", [128, NT, CS_COLS])
        p.dma("sp", dtr[:], COLS.rearrange("(n p) c -> p n c", p=128))
        xall = p.sb("xall", [128, NT, 8])
        tmpa = p.sb("tmpa", [128, NT, 8])
        dt_all = p.sb("dt_all", [128, NT, 8])
        dA_all = p.sb("dA_all", [128, NT, 8])
        p.tt("dve", xall[:], dtr[:, :, CS_DT:CS_DT + 8], dtb[:].broadcast_to([128, NT, 8]), ALU.add)
        softplus_small(p, dt_all[:], xall[:], tmpa[:])
        p.tt("dve", dA_all[:], dt_all[:], a_bc[:].broadcast_to([128, NT, 8]), ALU.mult)
        Sf = p.sb("Sf", [128, 4, 64])
        Sz = [[p.sb("Sz%d_%d" % (i, gg), [128, 256], BF16) for gg in range(2)] for i in range(2)]
        p.memset("dve", Sf[:], 0.0)
        for i in range(2):
            for gg in range(2):
                p.memset("dve", Sz[i][gg][:], 0.0)
        cm_z = [p.sb("cmz%d" % gg, [128, 512], BF16) for gg in range(2)]
        xdte_z = [p.sb("xdtez%d" % ch, [128, 8, 64], BF16) for ch in range(2)]
        for gg in range(2):
            p.memset("dve", cm_z[gg][:], 0.0)
            p.memset("dve", xdte_z[gg][:], 0.0)
        xw = [p.sb("xw%d" % i, [128, 6, 515]) for i in range(2)]
        acc = p.sb("acc", [128, 512])
        xbcs = p.sb("xbcs", [128, 6, 512])
        bc_bf = p.sb("bc_bf", [128, 2, 512], BF16)
        ct = [p.sb("ct%d" % i, [128, 512]) for i in range(2)]
        xs_t = p.sb("xs_t", [128, 8, 64])
        bm_t = p.sb("bm_t", [128, 128], BF16)
        ac = p.sb("ac", [128, 24])
        E3 = p.sb("E3", [128, 24, 1])
        dte2 = p.sb("dte2", [128, 8, 1])
        R = p.sb("R", [128, 8, 128])
        Eseg = p.sb("Eseg", [128, 8, 128])
        cbm = p.sb("cbm", [128, 2, 128])
        G = p.sb("G", [128, 8, 128], BF16)
        xdt = p.sb("xdt", [128, 8, 64], BF16)
        dec = p.sb("dec", [128, 2, 4, 1])
        t1 = p.sb("t1", [128, 8, 64])
        t2 = p.sb("t2", [128, 8, 64])
        zs = p.sb("zs", [128, 512])
        junk = p.sb("junk", [128, 256])
        st = p.sb("st", [128, 4])
        yn = p.sb("yn", [128, 512], BF16)
        yg = [p.sb("yg%d" % i, [128, 4, 512], BF16) for i in range(2)]
        pX = p.ps("pX")
        pM = p.ps("pM")
        pS = [p.ps("pS%d" % i) for i in range(2)]
        pY = p.ps("pY")
        pO = p.ps("pO")
        pCS = p.ps("pCS")
        pXb = pX[:].bitcast(BF16)
        pconv = [p.ps("pconv"), pCS]
        dg = p.sb("dg", [128, 24, 128])
        for i in range(24):
            p.ts("dve", dg[:, i, :], ident[:], pp[:, PP_SCW + i:PP_SCW + i + 1], ALU.mult)
        dA3 = p.sb("dA3", [128, 8, 1])
        dt3 = p.sb("dt3", [128, 8, 1])
        for g in range(S // 512):
            c0 = g * 512
            w_ = xw[g % 2]
            p.dma("sp", w_[:, :, 3:515], COLF[CF_XBC:CF_XBC + 768, c0:c0 + 512].rearrange("(c p) t -> p c t", p=128))
            if g == 0:
                p.memset("dve", w_[:, :, 0:3], 0.0)
            else:
                with p.nc.allow_non_contiguous_dma(reason="3-token conv halo"):
                    p.dma("sp", w_[:, :, 0:3], COLF[CF_XBC:CF_XBC + 768, c0 - 3:c0].rearrange("(c p) t -> p c t", p=128))
            for c in range(6):
                pcv = pconv[c % 2]
                for k in range(4):
                    p.mm(pcv[:], dg[:, c * 4 + k, :], w_[:, c, k:k + 512], start=(k == 0), stop=(k == 3))
                p.act(xbcs[:, c, :], pcv[:], AF.Silu, bias=pp[:, PP_SCB + c:PP_SCB + c + 1])
            p.copy("dve", bc_bf[:], xbcs[:, 4:6, :])
            for gg in range(2):
                p.copy("dve", cm_z[gg][gg * 64:(gg + 1) * 64, :], xbcs[gg * 64:(gg + 1) * 64, 5, :])
            ygb = yg[g % 2]
            for tt in range(4):
                ti = g * 4 + tt
                tl = slice(tt * 128, (tt + 1) * 128)
                ctb = ct[tt % 2]
                p.dma("sp", ctb[:], COLT[c0 + tt * 128:c0 + (tt + 1) * 128, CT_ZS:CT_ZS + 512])
                if DEBUG.get('SSD', 99) < 2:
                    continue
                for c in range(4):
                    p.tr(pX[:, c * 128:(c + 1) * 128], xbcs[:, c, tl], ident[:])
                p.copy("act", xs_t[:].rearrange("p h d -> p (h d)"), pX[:])
                p.tr(pM[:, 0:128], xbcs[:, 4, tl], ident[:])
                p.copy("dve", bm_t[:], pM[:, 0:128])
                if DEBUG.get('SSD', 99) < 3:
                    continue
                p.copy("dve", dA3[:, :, 0], dA_all[:, ti, :])
                p.mm(pM[:, 128:136], TRI, dA_all[:, ti, :])
                p.mm(pM[:, 136:144], BLK, dA_all[:, ti, :])
                p.copy("dve", ac[:, 0:8], pM[:, 128:136])
                p.copy("dve", ac[:, 16:24], pM[:, 136:144])
                p.tt("dve", ac[:, 8:16], ac[:, 16:24], ac[:, 0:8], ALU.subtract)
                p.act(E3[:, :, 0], ac[:], AF.Exp)
                p.tt("dve", dte2[:, :, 0], dt_all[:, ti, :], E3[:, 8:16, 0], ALU.mult)
                if DEBUG.get('SSD', 99) < 4:
                    continue
                p.tt("dve", R[:], tri3[:].broadcast_to([128, 8, 128]), dA3[:].broadcast_to([128, 8, 128]), ALU.mult)
                for hh in range(2):
                    p.mm(pS[hh][:], SUP, R[:, hh * 4:(hh + 1) * 4, :].rearrange("p h l -> p (h l)"))
                    p.act(Eseg[:, hh * 4:(hh + 1) * 4, :].rearrange("p h l -> p (h l)"), pS[hh][:], AF.Exp)
                if DEBUG.get('SSD', 99) < 5:
                    continue
                for gg in range(2):
                    p.mm(pM[:, 256 + gg * 128:256 + (gg + 1) * 128], bc_bf[:, 0, tl], cm_z[gg][:, tl])
                p.tt("dve", cbm[:], pM[:, 256:512].rearrange("p (g l) -> p g l", g=2), tri3[:].broadcast_to([128, 2, 128]), ALU.mult)
                for gg in range(2):
                    p.tt("dve", G[:, gg * 4:(gg + 1) * 4, :], Eseg[:, gg * 4:(gg + 1) * 4, :],
                         cbm[:, gg:gg + 1, :].broadcast_to([128, 4, 128]), ALU.mult)
                if DEBUG.get('SSD', 99) < 6:
                    continue
                p.copy("dve", dt3[:, :, 0], dt_all[:, ti, :])
                p.tt("dve", xdt[:], xs_t[:], dt3[:].broadcast_to([128, 8, 64]), ALU.mult)
                for ch in range(2):
                    hs = slice(ch * 64, (ch + 1) * 64)
                    p.tt("dve", xdte_z[ch][hs], xs_t[hs], dte2[hs].broadcast_to([64, 8, 64]), ALU.mult)
                for h in range(8):
                    p.mm(pY[:, h * 64:(h + 1) * 64], G[:, h, :], xdt[:, h, :])
                if DEBUG.get('SSD', 99) < 7:
                    continue
                for ch in range(2):
                    for gg in range(2):
                        p.mm(pCS[gg * 64:(gg + 1) * 64, ch * 256:(ch + 1) * 256],
                             bm_t[:, gg * 64:(gg + 1) * 64],
                             xdte_z[ch][:, gg * 4:(gg + 1) * 4, :].rearrange("p h d -> p (h d)"))
                for ch in range(2):
                    for gg in range(2):
                        sel = cst[:, C_SEL + (ch * 2 + gg) * 128:C_SEL + (ch * 2 + gg + 1) * 128]
                        p.mm(pM[:, 144 + ch * 4:144 + ch * 4 + 4], sel, E3[:, 16 + gg * 4:16 + gg * 4 + 4, 0],
                             start=(gg == 0), stop=(gg == 1))
                p.copy("dve", dec[:].rearrange("p c h a -> p (c h a)"), pM[:, 144:152])
                if DEBUG.get('SSD', 99) < 8:
                    continue
                for ch in range(2):
                    for gg in range(2):
                        p.mm(pO[ch * 64:(ch + 1) * 64, gg * 256:(gg + 1) * 256],
                             bc_bf[:, 1, tt * 128 + ch * 64:tt * 128 + (ch + 1) * 64], Sz[ch % 2][gg][:])
                    p.tt("dve", Sf[:], Sf[:], dec[:, ch, :, :].broadcast_to([128, 4, 64]), ALU.mult)
                    p.tt("dve", Sf[:], Sf[:], pCS[:, ch * 256:(ch + 1) * 256].rearrange("p (h d) -> p h d", h=4), ALU.add)
                    for gg in range(2):
                        p.copy("act" if gg == 0 else "dve", Sz[(ch + 1) % 2][gg][gg * 64:(gg + 1) * 64, :],
                               Sf[gg * 64:(gg + 1) * 64].rearrange("p h d -> p (h d)"))
                if DEBUG.get('SSD', 99) < 9:
                    continue
                p.tt("dve", t1[:], pO[:].rearrange("p (h d) -> p h d", h=8), E3[:, 0:8, :].broadcast_to([128, 8, 64]), ALU.mult)
                p.tt("dve", t1[:], t1[:], pY[:].rearrange("p (h d) -> p h d", h=8), ALU.add)
                p.tt("pool", t2[:], xs_t[:], dsk[:].broadcast_to([128, 8, 64]), ALU.mult)
                p.tt("dve", t1[:], t1[:], t2[:], ALU.add)
                if DEBUG.get('SSD', 99) < 10:
                    continue
                p.act(zs[:], ctb[:], AF.Silu)
                t1f = t1[:].rearrange("p h d -> p (h d)")
                p.tt("dve", t1f, t1f, zs[:], ALU.mult)
                for gg in range(2):
                    p.act(junk[:], t1f[:, gg * 256:(gg + 1) * 256], AF.Square, accum=st[:, gg:gg + 1])
                p.rstd(st[:, 2:4], st[:, 0:2], 1.0 / 256)
                for gg in range(2):
                    p.stt(yn[:, gg * 256:(gg + 1) * 256], t1f[:, gg * 256:(gg + 1) * 256], st[:, 2 + gg:3 + gg],
                          pp[:, PP_SNG + gg * 256:PP_SNG + (gg + 1) * 256], ALU.mult, ALU.mult)
                if DEBUG.get('SSD', 99) < 11:
                    continue
                for c in range(4):
                    p.tr(pXb[:, c * 128:(c + 1) * 128], yn[:, c * 128:(c + 1) * 128], identb[:])
                p.copy("act", ygb[:, :, tl], pXb[:, 0:512].rearrange("p (c t) -> p c t", c=4))
            p.dma("sp", YT[0:512, c0:c0 + 512].rearrange("(c p) t -> p c t", p=128), ygb[:])


def rope_phase(ctx, W, pos, ROPE):
    S = ctx.S
    TWO_PI = 2.0 * np.pi * (1.0 - 2e-6)
    with ctx.phase() as p:
        cst = p.sb("cst", [128, NCST])
        p.dma("sp", cst[:], W["cst"])
        posi = p.sb("posi", [32, S], I32)
        p.dma("sp", posi[:], pos.partition_broadcast(32))
        t = p.sb("t", [32, S])
        ti = p.sb("ti", [32, S], I32)
        tf = p.sb("tf", [32, S])
        f = p.sb("f", [32, S])
        m = p.sb("m", [32, S])
        o = p.sb("o", [32, S])
        pib = p.sb("pib", [32, 1])
        p.memset("dve", pib[:], float(np.pi * (1.0 - 2e-6)))
        p.copy("dve", t[:], posi[:])
        p.ts("dve", t[:], t[:], cst[0:32, C_INVF:C_INVF + 1], ALU.mult)
        for which in (1, 0):
            if which == 0:
                p.ts("dve", t[:], t[:], 0.25, ALU.add)
            p.copy("dve", ti[:], t[:])
            p.copy("dve", tf[:], ti[:])
            p.tt("dve", f[:], t[:], tf[:], ALU.subtract)
            p.ts("dve", m[:], f[:], 0.0, ALU.is_lt)
            p.tt("dve", f[:], f[:], m[:], ALU.add)
            p.ts("dve", m[:], f[:], 1.0, ALU.is_ge)
            p.tt("dve", f[:], f[:], m[:], ALU.subtract)
            p.act(o[:], f[:], AF.Sin, scale=-TWO_PI, bias=pib[:])
            p.dma("sp", ROPE[which], o[:])


def mla_phase(ctx, W, L, COLF, ROPE, YT):
    S = ctx.S
    NT = S // 128
    NG = S // 512
    SCALE = 96.0 ** -0.5
    with ctx.phase() as p:
        pp = p.sb("pp", [128, NPP])
        p.dma("sp", pp[:], W["pp"][L])
        cst = p.sb("cst", [128, NCST])
        p.dma("sp", cst[:], W["cst"])
        ones = p.sb("ones", [128, 128], BF16)
        p.memset("dve", ones[:], 1.0)
        mkq = p.sb("mkq", [128, 128], BF16)
        p.copy("dve", mkq[:], cst[:, C_MKQ:C_MKQ + 128])
        wuq = p.sb("wuq", [128, 2, 384], BF16)
        p.dma("pool", wuq[:], W["mla_w_uq"][L].rearrange("(c p) f -> p c f", p=128))
        wukv = p.sb("wukv", [128, 512], BF16)
        p.dma("pool", wukv[:], W["mla_w_ukv"][L])
        wrot = p.sb("wrot", [128, 2, 4, 32], BF16)
        wv = p.sb("wv", [128, 4, 64], BF16)
        for h in range(4):
            b = h * 96 + 64
            p.ts("dve", wrot[:, :, h, 0:16], wuq[:, :, b + 16:b + 32], -1.0, ALU.mult)
            p.copy("dve", wrot[:, :, h, 16:32], wuq[:, :, b:b + 16])
            p.copy("dve", wv[:, h, :], wukv[:, h * 128 + 64:h * 128 + 128])
        kT = [p.sb("kT%d" % h, [128, S], BF16) for h in range(4)]
        qT = [p.sb("qT%d" % h, [128, 512], BF16) for h in range(4)]
        for h in range(4):
            p.memset("dve", kT[h][:], 0.0)
            p.memset("dve", qT[h][:], 0.0)
        vt = p.sb("vt", [128, NT, 4, 128], BF16)
        p.memset("dve", vt[:], 0.0)
        ckvn = p.sb("ckvn", [128, 512], BF16)
        cin = p.sb("cin", [128, 2, 512])
        sq = p.sb("sq", [128, 2, 512], BF16)
        cqn = p.sb("cqn", [128, 2, 512], BF16)
        rbc = p.sb("rbc", [128, 512])
        cs2 = p.sb("cs2", [128, 2, 512])
        kr = p.sb("kr", [128, 2, 512])
        r1 = p.sb("r1", [128, 512])
        r2 = p.sb("r2", [128, 512])
        krf = p.sb("krf", [128, 512], BF16)
        PT = [p.sb("PT%d" % i, [128, 512], BF16) for i in range(3)]
        rb = p.sb("rb", [128, 512])
        oT = [p.sb("oT%d" % i, [128, 512], BF16) for i in range(2)]
        pss = p.ps("pss")
        pq = [p.ps("pq%d" % i) for i in range(3)]
        ps = [p.ps("ps%d" % i) for i in range(2)]
        pA = p.ps("pA")
        pB = p.ps("pB")
        R32 = slice(64, 96)
        ps3 = [ps[0], ps[1], pq[2]]

        def load_cs(c0):
            p.dma("sp", cs2[R32, 0, :], ROPE[0, :, c0:c0 + 512])
            p.dma("sp", cs2[R32, 1, :], ROPE[1, :, c0:c0 + 512])

        for g in range(NG):
            c0 = g * 512
            cols = slice(c0, c0 + 512)
            load_cs(c0)
            p.dma("sp", cin[:, 0, :], COLF[CF_CKV:CF_CKV + 128, cols])
            p.act(sq[:, 0, :], cin[:, 0, :], AF.Square)
            p.mm(pss[:], ones[:], sq[:, 0, :])
            p.rstd(rbc[:], pss[:], 1.0 / 128)
            p.stt(ckvn[:], cin[:, 0, :], pp[:, PP_KVG:PP_KVG + 1], rbc[:], ALU.mult, ALU.mult)
            for h in range(4):
                p.mm(pq[h % 2][0:64, :], wukv[:, h * 128:h * 128 + 64], ckvn[:])
                p.copy("act" if h % 2 == 0 else "dve", kT[h][0:64, cols], pq[h % 2][0:64, :])
            for tt in range(4):
                p.mm(pq[2][:, 0:256], ckvn[:, tt * 128:(tt + 1) * 128], wv[:].rearrange("p h d -> p (h d)"))
                p.copy("act" if tt % 2 == 0 else "dve", vt[:, g * 4 + tt, :, 0:64], pq[2][:, 0:256].rearrange("p (h d) -> p h d", h=4))
            p.dma("sp", kr[R32, 0, :], COLF[CF_KR:CF_KR + 32, cols])
            p.dma("sp", kr[64:80, 1, :], COLF[CF_KR + 16:CF_KR + 32, cols])
            p.dma("sp", kr[80:96, 1, :], COLF[CF_KR:CF_KR + 16, cols])
            p.ts("dve", kr[64:80, 1, :], kr[64:80, 1, :], -1.0, ALU.mult)
            p.tt("dve", r1[R32], kr[R32, 0, :], cs2[R32, 0, :], ALU.mult)
            p.tt("dve", r2[R32], kr[R32, 1, :], cs2[R32, 1, :], ALU.mult)
            p.tt("dve", krf[R32], r1[R32], r2[R32], ALU.add)
            for h in range(4):
                p.copy("act" if h % 2 == 0 else "dve", kT[h][R32, cols], krf[R32])
        for G in range(NG):
            c0 = G * 512
            load_cs(c0)
            p.dma("sp", cin[:], COLF[CF_CQ:CF_CQ + 256, c0:c0 + 512].rearrange("(c p) t -> p c t", p=128))
            p.act(sq[:], cin[:], AF.Square)
            for kc in range(2):
                p.mm(pss[:], ones[:], sq[:, kc, :], start=(kc == 0), stop=(kc == 1))
            p.rstd(rbc[:], pss[:], 1.0 / 256)
            for kc in range(2):
                p.stt(cqn[:, kc, :], cin[:, kc, :], pp[:, PP_QG + kc:PP_QG + kc + 1], rbc[:], ALU.mult, ALU.mult)
            for h in range(4):
                b = h * 96
                for kc in range(2):
                    p.mm(pq[0][0:64, :], wuq[:, kc, b:b + 64], cqn[:, kc, :], start=(kc == 0), stop=(kc == 1))
                for kc in range(2):
                    p.mm(pq[1][R32, :], wuq[:, kc, b + 64:b + 96], cqn[:, kc, :], start=(kc == 0), stop=(kc == 1))
                for kc in range(2):
                    p.mm(pq[2][R32, :], wrot[:, kc, h, :], cqn[:, kc, :], start=(kc == 0), stop=(kc == 1))
                p.act(qT[h][0:64, :], pq[0][0:64, :], AF.Copy, scale=SCALE)
                p.tt("dve", r1[R32], pq[1][R32, :], cs2[R32, 0, :], ALU.mult)
                p.tt("dve", r2[R32], pq[2][R32, :], cs2[R32, 1, :], ALU.mult)
                p.tt("dve", r1[R32], r1[R32], r2[R32], ALU.add)
                p.ts("dve", qT[h][R32, :], r1[R32], SCALE, ALU.mult)
            for h in range(4):
                nkt = 4 * G + 4
                accA, accB = (pA, pB) if h % 2 == 0 else (pq[0], pq[1])

                def score(kt, h=h, G=G):
                    r = kt - 4 * G
                    q0 = max(r, 0) * 128
                    psb, ptb = ps3[kt % 3], PT[kt % 3]
                    p.mm(psb[:, q0:512], kT[h][:, kt * 128:(kt + 1) * 128], qT[h][:, q0:512])
                    p.act(ptb[:, q0:512], psb[:, q0:512], AF.Exp)
                    if r >= 0:
                        p.tt("dve", ptb[:, q0:q0 + 128], ptb[:, q0:q0 + 128], mkq[:], ALU.mult)

                def pv(kt, h=h, G=G, nkt=nkt, accA=accA, accB=accB):
                    r = kt - 4 * G
                    q0 = max(r, 0) * 128
                    ptb = PT[kt % 3]
                    p.mm(accA[:, q0:512], vt[:, kt, h, :], ptb[:, q0:512], start=(kt == 0), stop=(kt == nkt - 1))
                    p.mm(accB[:, q0:512], ones[:], ptb[:, q0:512], start=(kt == 0), stop=(kt == nkt - 1))

                score(0)
                if nkt > 1:
                    score(1)
                for kt in range(nkt):
                    if kt + 2 < nkt:
                        score(kt + 2)
                    pv(kt)
                p.recip(rb[0:64, :], accB[0:64, :])
                ob = oT[h % 2]
                p.tt("dve", ob[0:64, :], accA[0:64, :], rb[0:64, :], ALU.mult)
                p.dma("sp", YT[512 + h * 64:512 + (h + 1) * 64, c0:c0 + 512], ob[0:64, :])


def gdn_phase(ctx, W, L, COLF, COLT, COLS, YT):
    S = ctx.S
    NT = S // 128
    with ctx.phase() as p:
        pp = p.sb("pp", [128, NPP])
        p.dma("sp", pp[:], W["pp"][L])
        ident = p.sb("ident", [128, 128])
        p.dma("sp", ident[:], W["ident"])
        cst = p.sb("cst", [128, NCST])
        p.dma("sp", cst[:], W["cst"])
        TRI = cst[:, C_TRI:C_TRI + 128]
        BLK = cst[:, C_BLK:C_BLK + 128]
        identb = p.sb("identb", [128, 1, 128], BF16)
        p.copy("dve", identb[:, 0, :], cst[:, C_IDB:C_IDB + 128])
        blkb = p.sb("blkb", [128, 128], BF16)
        p.copy("dve", blkb[:], BLK)
        sup3 = p.sb("sup3", [128, 1, 128])
        p.copy("dve", sup3[:, 0, :], cst[:, C_SUP:C_SUP + 128])
        linc3 = p.sb("linc3", [128, 1, 128])
        p.copy("dve", linc3[:, 0, :], cst[:, C_LINC:C_LINC + 128])
        ngb = p.sb("ngb", [128, 1, 64])
        p.copy("dve", ngb[:, 0, :], pp[:, PP_GNG:PP_GNG + 64])
        dtr = p.sb("dtr", [128, NT, CS_COLS])
        p.dma("sp", dtr[:], COLS.rearrange("(n p) c -> p n c", p=128))
        na_bc = p.sb("na_bc", [128, 1, 4])
        p.act(na_bc[:, 0, :], pp[:, PP_GALOG:PP_GALOG + 4], AF.Exp)
        p.ts("dve", na_bc[:, 0, :], na_bc[:, 0, :], -1.0, ALU.mult)
        dtb = p.sb("dtb", [128, 1, 4])
        p.copy("dve", dtb[:, 0, :], pp[:, PP_GDTB:PP_GDTB + 4])
        xall = p.sb("xall", [128, NT, 4])
        tmpa = p.sb("tmpa", [128, NT, 4])
        g_all = p.sb("g_all", [128, NT, 4])
        beta_all = p.sb("beta_all", [128, NT, 4])
        p.tt("dve", xall[:], dtr[:, :, CS_A:CS_A + 4], dtb[:].broadcast_to([128, NT, 4]), ALU.add)
        softplus_small(p, g_all[:], xall[:], tmpa[:])
        p.tt("dve", g_all[:], g_all[:], na_bc[:].broadcast_to([128, NT, 4]), ALU.mult)
        p.act(beta_all[:], dtr[:, :, CS_B:CS_B + 4], AF.Sigmoid)
        Sf = [p.sb("Sf%d" % c, [128, 64]) for c in range(2)]
        Sb = [p.sb("Sb%d" % c, [128, 64], BF16) for c in range(2)]
        for c in range(2):
            p.memset("dve", Sf[c][:], 0.0)
            p.memset("dve", Sb[c][:], 0.0)
        xw = [p.sb("xw%d" % i, [128, 6, 515]) for i in range(2)]
        acc = p.sb("acc", [128, 512])
        qkv = p.sb("qkv", [128, 6, 512])
        sq = p.sb("sq", [128, 512], BF16)
        rin = p.sb("rin", [128, 512])
        nT = p.sb("nT", [128, 4, 512], BF16)
        nTz = [p.sb("nTz%d" % i, [128, 512], BF16) for i in range(8)]
        for i in range(8):
            p.memset("dve", nTz[i][:], 0.0)
        zt = [p.sb("zt%d" % i, [128, 256]) for i in range(2)]
        kn_t = p.sb("kn_t", [128, 4, 64])
        v_t = p.sb("v_t", [128, 4, 64])
        g3 = p.sb("g3", [128, 4, 1])
        gc = p.sb("gc", [128, 12])
        E3 = p.sb("E3", [128, 12, 1])
        be = p.sb("be", [128, 4, 1])
        nb = p.sb("nb", [128, 4, 1])
        b3 = p.sb("b3", [128, 4, 1])
        R2 = p.sb("R2", [128, 4, 128])
        E = p.sb("E", [128, 4, 128])
        Gs = p.sb("Gs", [128, 4, 128])
        tmpK = p.sb("tmpK", [128, 4, 128])
        QKm = p.sb("QKm", [128, 4, 128], BF16)
        QKmT = p.sb("QKmT", [128, 4, 128], BF16)
        A = [p.sb("A%d" % i, [128, 4, 128]) for i in range(2)]
        B = [p.sb("B%d" % i, [128, 4, 128]) for i in range(2)]
        P = [p.sb("P%d" % i, [128, 4, 128]) for i in range(2)]
        bv = p.sb("bv", [128, 4, 64])
        bke = p.sb("bke", [128, 4, 64])
        kdz = [p.sb("kdz%d" % i, [128, 4, 64], BF16) for i in range(2)]
        wTz = p.sb("wTz", [128, 4, 128], BF16)
        vnew = p.sb("vnew", [128, 4, 64], BF16)
        for t_ in (kdz[0], kdz[1], wTz, vnew):
            p.memset("dve", t_[:], 0.0)
        u = p.sb("u", [128, 4, 64])
        o1 = p.sb("o1", [128, 4, 64])
        o = p.sb("o", [128, 4, 64])
        osq = p.sb("osq", [128, 4, 64])
        st = p.sb("st", [128, 8])
        zs = p.sb("zs", [128, 256])
        yn = p.sb("yn", [128, 256], BF16)
        dec = p.sb("dec", [128, 4])
        yg = [p.sb("yg%d" % i, [128, 2, 512], BF16) for i in range(2)]
        bk = [p.ps("b%d" % i) for i in range(8)]
        b0b = bk[0][:].bitcast(BF16)
        b1b = bk[1][:].bitcast(BF16)
        dg = p.sb("dg", [128, 24, 128])
        for i in range(24):
            p.ts("dve", dg[:, i, :], ident[:], pp[:, PP_GCW + i:PP_GCW + i + 1], ALU.mult)
        ident3 = p.sb("ident3", [128, 1, 128])
        p.copy("dve", ident3[:, 0, :], ident[:])
        for g in range(S // 512):
            c0 = g * 512
            w_ = xw[g % 2]
            p.dma("sp", w_[:, :, 3:515], COLF[CF_QKV:CF_QKV + 768, c0:c0 + 512].rearrange("(c p) t -> p c t", p=128))
            if g == 0:
                p.memset("dve", w_[:, :, 0:3], 0.0)
            else:
                with p.nc.allow_non_contiguous_dma(reason="3-token conv halo"):
                    p.dma("sp", w_[:, :, 0:3], COLF[CF_QKV:CF_QKV + 768, c0 - 3:c0].rearrange("(c p) t -> p c t", p=128))
            for c in range(6):
                pcv = bk[5 + c % 2]
                for k in range(4):
                    p.mm(pcv[:], dg[:, c * 4 + k, :], w_[:, c, k:k + 512], start=(k == 0), stop=(k == 3))
                p.act(qkv[:, c, :], pcv[:], AF.Silu)
            for c in range(4):
                p.act(sq[:], qkv[:, c, :], AF.Square)
                p.mm(bk[1][:], blkb[:], sq[:])
                p.rstd(rin[:], bk[1][:], 1.0)
                if c < 2:
                    p.stt(nT[:, c, :], qkv[:, c, :], 0.125, rin[:], ALU.mult, ALU.mult)
                else:
                    p.tt("dve", nT[:, c, :], qkv[:, c, :], rin[:], ALU.mult)
                for par in range(2):
                    hs = slice(par * 64, (par + 1) * 64)
                    hidx = (c % 2) * 2 + par + (0 if c < 2 else 4)
                    p.copy("act" if par == 0 else "dve", nTz[hidx][hs, :], nT[hs, c, :])
            ygb = yg[g % 2]
            for tt in range(4):
                ti = g * 4 + tt
                tl = slice(tt * 128, (tt + 1) * 128)
                ztb = zt[tt % 2]
                p.dma("sp", ztb[:], COLT[c0 + tt * 128:c0 + (tt + 1) * 128, CT_ZG:CT_ZG + 256])
                for c in range(2):
                    p.tr(b0b[:, c * 128:(c + 1) * 128], nT[:, 2 + c, tl], identb[:, 0, :])
                p.copy("act", kn_t[:].rearrange("p h d -> p (h d)"), b0b[:, 0:256])
                for c in range(2):
                    p.tr(bk[0][:, c * 128:(c + 1) * 128], qkv[:, 4 + c, tl], ident[:])
                p.copy("dve", v_t[:].rearrange("p h d -> p (h d)"), bk[0][:, 0:256])
                p.copy("dve", g3[:, :, 0], g_all[:, ti, :])
                p.copy("dve", b3[:, :, 0], beta_all[:, ti, :])
                p.mm(bk[1][:, 0:4], TRI, g_all[:, ti, :])
                p.mm(bk[1][:, 4:8], BLK, g_all[:, ti, :])
                p.copy("dve", gc[:, 0:4], bk[1][:, 0:4])
                p.copy("dve", gc[:, 8:12], bk[1][:, 4:8])
                p.tt("dve", gc[:, 4:8], gc[:, 8:12], gc[:, 0:4], ALU.subtract)
                p.act(E3[:, :, 0], gc[:], AF.Exp)
                p.tt("dve", be[:], b3[:], E3[:, 0:4, :], ALU.mult)
                p.ts("dve", nb[:], b3[:], -1.0, ALU.mult)
                p.tt("dve", R2[:], sup3[:].broadcast_to([128, 4, 128]), g3[:].broadcast_to([128, 4, 128]), ALU.mult)
                p.mm(bk[2][:], TRI, R2[:].rearrange("p h s -> p (h s)"))
                p.act(E[:].rearrange("p h s -> p (h s)"), bk[2][:], AF.Exp)
                for h in range(4):
                    p.mm(bk[3][:, h * 128:(h + 1) * 128], nTz[4 + h][:, tl], nTz[4 + h][:, tl])
                for h in range(4):
                    p.mm(bk[4][:, h * 128:(h + 1) * 128], nTz[h][:, tl], nTz[4 + h][:, tl])
                p.tt("dve", Gs[:], E[:], sup3[:].broadcast_to([128, 4, 128]), ALU.mult)
                p.tt("dve", tmpK[:], bk[3][:].rearrange("p (h s) -> p h s", h=4), nb[:].broadcast_to([128, 4, 128]), ALU.mult)
                p.tt("dve", A[0][:], tmpK[:], Gs[:], ALU.mult)
                p.tt("dve", Gs[:], E[:], linc3[:].broadcast_to([128, 4, 128]), ALU.mult)
                p.tt("dve", QKm[:], bk[4][:].rearrange("p (h s) -> p h s", h=4), Gs[:], ALU.mult)
                for h in range(4):
                    p.tr(bk[0][:, h * 128:(h + 1) * 128], A[0][:, h, :], ident[:])
                p.copy("act", B[0][:].rearrange("p h s -> p (h s)"), bk[0][:])
                p.tt("dve", P[0][:], B[0][:], ident3[:].broadcast_to([128, 4, 128]), ALU.add)
                for h in range(4):
                    p.tr(b1b[:, 512 + h * 128:512 + (h + 1) * 128], QKm[:, h, :], identb[:, 0, :])
                p.copy("act", QKmT[:].rearrange("p h s -> p (h s)"), b1b[:, 512:1024])
                for k in range(1, 6):
                    Ao, Bo = A[(k - 1) % 2], B[(k - 1) % 2]
                    An, Bn = A[k % 2], B[k % 2]
                    Po, Pn = P[(k - 1) % 2], P[k % 2]
                    for h in range(4):
                        p.mm(bk[5][:, h * 128:(h + 1) * 128], Bo[:, h, :], Ao[:, h, :])
                    if k < 5:
                        for h in range(4):
                            p.mm(bk[6][:, h * 128:(h + 1) * 128], Ao[:, h, :], Bo[:, h, :])
                    p.copy("act", An[:].rearrange("p h s -> p (h s)"), bk[5][:])
                    if k < 5:
                        p.copy("dve", Bn[:].rearrange("p h s -> p (h s)"), bk[6][:])
                    for h in range(4):
                        p.mm(bk[7][:, h * 128:(h + 1) * 128], An[:, h, :], Po[:, h, :])
                    p.tt("dve", Pn[:], bk[7][:].rearrange("p (h s) -> p h s", h=4), Po[:], ALU.add)
                PT_ = P[5 % 2]
                p.tt("dve", bv[:], v_t[:], b3[:].broadcast_to([128, 4, 64]), ALU.mult)
                p.tt("dve", bke[:], kn_t[:], be[:].broadcast_to([128, 4, 64]), ALU.mult)
                for ch in range(2):
                    hs = slice(ch * 64, (ch + 1) * 64)
                    p.tt("dve", kdz[ch][hs], kn_t[hs], E3[hs, 4:8, :].broadcast_to([64, 4, 64]), ALU.mult)
                for h in range(4):
                    p.mm(bk[2][:, h * 64:(h + 1) * 64], PT_[:, h, :], bv[:, h, :])
                p.copy("act", u[:].rearrange("p h d -> p (h d)"), bk[2][:, 0:256])
                for h in range(4):
                    ps_ = slice((h % 2) * 64, (h % 2 + 1) * 64)
                    p.mm(bk[3][ps_, h * 128:(h + 1) * 128], bke[:, h, :], PT_[:, h, :])
                for h in range(4):
                    ps_ = slice((h % 2) * 64, (h % 2 + 1) * 64)
                    p.copy("dve" if h % 2 == 0 else "act", wTz[ps_, h, :], bk[3][ps_, h * 128:(h + 1) * 128])
                for ch in range(2):
                    for par in range(2):
                        sel = cst[:, C_SEL + (ch * 2 + par) * 128:C_SEL + (ch * 2 + par + 1) * 128]
                        p.mm(bk[1][:, 8 + ch * 2:8 + ch * 2 + 2], sel, E3[:, 8 + par:12:2, 0],
                             start=(par == 0), stop=(par == 1))
                p.copy("dve", dec[:], bk[1][:, 8:12])
                for ch in range(2):
                    rs = slice(ch * 64, (ch + 1) * 64)
                    cl = slice(tt * 128 + ch * 64, tt * 128 + (ch + 1) * 64)
                    lc = slice(ch * 64, (ch + 1) * 64)
                    for h in range(4):
                        p.mm(bk[4][rs, h * 64:(h + 1) * 64], wTz[:, h, lc], Sb[h // 2][:])
                    for h in range(4):
                        p.mm(bk[5][rs, h * 64:(h + 1) * 64], nTz[h][:, cl], Sb[h // 2][:])
                    p.tt("dve", vnew[rs].rearrange("p h d -> p (h d)"), u[rs].rearrange("p h d -> p (h d)"),
                         bk[4][rs, 0:256], ALU.subtract)
                    p.tt("dve", o1[rs], bk[5][rs, 0:256].rearrange("p (h d) -> p h d", h=4),
                         E3[rs, 0:4, :].broadcast_to([64, 4, 64]), ALU.mult)
                    for h in range(4):
                        p.mm(bk[6][rs, h * 64:(h + 1) * 64], QKmT[:, h, lc], vnew[:, h, :])
                    for h in range(4):
                        ps_ = slice((h % 2) * 64, (h % 2 + 1) * 64)
                        p.mm(bk[7][ps_, (h // 2) * 64:(h // 2 + 1) * 64], kdz[ch][:, h, :], vnew[:, h, :])
                    p.tt("dve", o[rs].rearrange("p h d -> p (h d)"), o1[rs].rearrange("p h d -> p (h d)"),
                         bk[6][rs, 0:256], ALU.add)
                    for c in range(2):
                        p.stt(Sf[c][:], Sf[c][:], dec[:, ch * 2 + c:ch * 2 + c + 1], bk[7][:, c * 64:(c + 1) * 64],
                              ALU.mult, ALU.add)
                        p.copy("act", Sb[c][:], Sf[c][:])
                p.tt("dve", osq[:], o[:], o[:], ALU.mult)
                p.s.add("dve", lambda e: e.tensor_reduce(st[:, 0:4], osq[:], AX.X, ALU.add), reads=[osq[:]], writes=[st[:, 0:4]])
                p.rstd(st[:, 4:8], st[:, 0:4], 1.0 / 64)
                p.copy("dve", nb[:, :, 0], st[:, 4:8])
                p.tt("dve", o[:], o[:], nb[:].broadcast_to([128, 4, 64]), ALU.mult)
                p.tt("dve", o[:], o[:], ngb[:].broadcast_to([128, 4, 64]), ALU.mult)
                p.act(zs[:], ztb[:], AF.Silu)
                p.tt("dve", yn[:], o[:].rearrange("p h d -> p (h d)"), zs[:], ALU.mult)
                for c in range(2):
                    p.tr(b0b[:, c * 128:(c + 1) * 128], yn[:, c * 128:(c + 1) * 128], identb[:, 0, :])
                p.copy("act", ygb[:, :, tl], b0b[:, 0:256].rearrange("p (c t) -> p c t", c=2))
            p.dma("sp", YT[768:1024, c0:c0 + 512].rearrange("(c p) t -> p c t", p=128), ygb[:])


WEIGHT_NAMES = ["ffn_w_up", "ffn_w_down", "w_in", "mla_w_uq", "mla_w_ukv", "w_out", "xa_w_q", "xa_w_kv", "xa_w_o"]


def build_program(S, depth, shapes):
    nc = bass.Bass("TRN2", target_bir_lowering=False)
    x = nc.dram_tensor("x", [S, D], F32, kind="ExternalInput").ap()
    mem = nc.dram_tensor("mem", [N_MEM, D], F32, kind="ExternalInput").ap()
    pos = nc.dram_tensor("positions", [S], I32, kind="ExternalInput").ap()
    W = {}
    for nm in WEIGHT_NAMES:
        W[nm] = nc.dram_tensor(nm, list(shapes[nm]), F32, kind="ExternalInput").ap()
    W["pp"] = nc.dram_tensor("pp", [depth, 128, NPP], F32, kind="ExternalInput").ap()
    W["cst"] = nc.dram_tensor("cst", [128, NCST], F32, kind="ExternalInput").ap()
    W["ident"] = nc.dram_tensor("ident", [128, 128], F32, kind="ExternalInput").ap()
    y = nc.dram_tensor("y", [S, D], F32, kind="ExternalOutput").ap()
    HT = nc.dram_tensor("HT", [D, S], F32).ap()
    COLF = nc.dram_tensor("COLF", [CF_ROWS, S], F32).ap()
    COLT = nc.dram_tensor("COLT", [S, CT_COLS], F32).ap()
    COLS = nc.dram_tensor("COLS", [S, CS_COLS], F32).ap()
    YT = nc.dram_tensor("YT", [D, S], BF16).ap()
    ROPE = nc.dram_tensor("ROPE", [2, 32, S], F32).ap()
    ctx = Ctx(nc, S)
    with ctx.stack:
        to_feature_major(ctx, W, x, HT)
        rope_phase(ctx, W, pos, ROPE)
        for L in range(depth):
            ffn_phase(ctx, W, L, 0, HT, HT)
            inproj_phase(ctx, W, L, HT, COLF, COLT, COLS)
            ssd_phase(ctx, W, L, COLF, COLT, COLS, YT)
            mla_phase(ctx, W, L, COLF, ROPE, YT)
            gdn_phase(ctx, W, L, COLF, COLT, COLS, YT)
            outproj_phase(ctx, W, L, HT, YT)
            xattn_phase(ctx, W, L, HT, mem)
            ffn_phase(ctx, W, L, 1, HT, HT)
        to_token_major(ctx, W, HT, y)
    return nc


def kernel(**inputs):
    inp = {k: np.asarray(v) for k, v in inputs.items()}
    B, S, _ = inp["x"].shape
    depth = inp["norm_g"].shape[0]
    shapes = {nm: inp[nm].shape for nm in WEIGHT_NAMES}
    nc = build_program(S, depth, shapes)
    shared = {nm: np.ascontiguousarray(inp[nm], dtype=np.float32) for nm in WEIGHT_NAMES}
    shared["pp"] = make_pp(inp, depth)
    shared["cst"] = make_cst()
    shared["ident"] = np.eye(128, dtype=np.float32)
    in_maps = []
    for b in range(B):
        m = dict(shared)
        m["x"] = np.ascontiguousarray(inp["x"][b], dtype=np.float32)
        m["mem"] = np.ascontiguousarray(inp["mem"][b], dtype=np.float32)
        m["positions"] = np.ascontiguousarray(inp["positions"][b], dtype=np.int32)
        in_maps.append(m)
    res = run_bass_kernel_spmd(nc, in_maps, core_ids=list(range(B)))
    return np.stack([np.asarray(r["y"], dtype=np.float32) for r in res.results], axis=0)
```
